# Optimizing a Trainium2 kernel written in Bass

```python
import jax, jax.numpy as jnp
from jax import lax
import numpy as np

D_MODEL = 1024
BATCH = 4
SEQ = 4096
DEPTH = 4
DEC_BATCH = 32
DEC_SEQ = 8
PAST_LEN = 8192
PAGE_SIZE = 128

N_MIXERS = 4
GROUP_W = D_MODEL // N_MIXERS
HEAD_DIM = 64
N_HEADS = GROUP_W // HEAD_DIM
D_MIX = N_MIXERS * GROUP_W
RW_W_RANK = 32
RW_A_RANK = 32
RW_G_RANK = 64
RW_COLS = 3 * GROUP_W + RW_W_RANK + RW_A_RANK + RW_G_RANK
MOBA_COLS = 3 * GROUP_W
CONV_COLS = 3 * GROUP_W
CMLP_COLS = 2 * GROUP_W
N_COLS = RW_COLS + MOBA_COLS + CONV_COLS + CMLP_COLS
MOBA_BLOCK = 256
MOBA_TOPK = 3
Q_BLOCK = 128
CONV_W = 3
CHUNK = 128
PEER_KEYS = 128
PEER_EXPERTS = PEER_KEYS * PEER_KEYS
PEER_HEADS = 8
PEER_TOPK = 16
PEER_QDIM = 256
PEER_HALF = PEER_QDIM // 2
PEER_TOKEN_BLOCK = 256
ALPHA = (2.0 * DEPTH) ** 0.25
BETA = (8.0 * DEPTH) ** -0.25
LN_EPS = 1e-5
GN_EPS = 64e-5

kernel_name = 'hybrid_rwkv7_moba_conv_gmlp_peer_step'

F32 = jnp.float32


def layer_norm(x, g, b, eps=LN_EPS):
    xf = x.astype(F32)
    mu = jnp.mean(xf, -1, keepdims=True)
    var = jnp.mean(jnp.square(xf - mu), -1, keepdims=True)
    return ((xf - mu) * lax.rsqrt(var + eps) * g.astype(F32) + b.astype(F32)).astype(x.dtype)


def rwkv7_mix(p, shift_prev, s0, mu, w0, w2, a0, a2, g2, k_k, k_a, r_k, lnx_g, lnx_b):
    Bn, T, _ = p.shape
    G = GROUP_W
    prev = jnp.concatenate([shift_prev[:, None].astype(p.dtype), p[:, :-1]], axis=1)
    xm = p + (prev - p) * mu
    r, k, v = xm[..., :G], xm[..., G:2 * G], xm[..., 2 * G:3 * G]
    o = 3 * G
    wl = xm[..., o:o + RW_W_RANK]
    o += RW_W_RANK
    al = xm[..., o:o + RW_A_RANK]
    o += RW_A_RANK
    gl = xm[..., o:o + RW_G_RANK]
    w = -jax.nn.softplus(-(w0 + jnp.tanh(wl) @ w2)) - 0.5
    decay = jnp.exp(-jnp.exp(w.astype(F32)))
    a = jax.nn.sigmoid(a0 + al @ a2)
    g = jax.nn.sigmoid(gl) @ g2
    heads = lambda t: t.reshape(Bn, T, N_HEADS, HEAD_DIM).astype(F32)
    kk = heads(k * k_k)
    kk = kk / jnp.maximum(jnp.linalg.norm(kk, axis=-1, keepdims=True), 1e-12)
    k = k * (1 + (a - 1) * k_a)
    rh, kh, vh, wh, ah = heads(r), heads(k), heads(v), heads(decay), heads(a)

    def step(S, inp):
        r_t, w_t, k_t, v_t, kk_t, a_t = inp
        sa = jnp.einsum('bhvk,bhk->bhv', S, -kk_t)
        S = (S * w_t[:, :, None, :] + sa[..., None] * (kk_t * a_t)[:, :, None, :]
             + v_t[..., None] * k_t[:, :, None, :])
        return S, jnp.einsum('bhvk,bhk->bhv', S, r_t)

    xs = tuple(jnp.moveaxis(t, 1, 0) for t in (rh, wh, kh, vh, kk, ah))
    s_fin, ys = lax.scan(step, s0.astype(F32), xs)
    y = jnp.moveaxis(ys, 0, 1)
    ym = jnp.mean(y, -1, keepdims=True)
    yv = jnp.mean(jnp.square(y - ym), -1, keepdims=True)
    yn = ((y - ym) * lax.rsqrt(yv + GN_EPS)).reshape(Bn, T, G) * lnx_g + lnx_b
    bonus = (jnp.sum(rh * kh * r_k, -1, keepdims=True) * vh).reshape(Bn, T, G)
    out = (yn + bonus) * g
    return out.astype(p.dtype), s_fin.astype(s0.dtype), p[:, -1]


def moba_prompt(q, k, v):
    Bn, S = q.shape[:2]
    n_blk = -(-S // MOBA_BLOCK)
    pad = ((0, 0), (0, n_blk * MOBA_BLOCK - S), (0, 0), (0, 0))
    kp, vp = jnp.pad(k, pad), jnp.pad(v, pad)
    kblk = kp.reshape(Bn, n_blk, MOBA_BLOCK, N_HEADS, HEAD_DIM)
    vblk = vp.reshape(Bn, n_blk, MOBA_BLOCK, N_HEADS, HEAD_DIM)
    k_mean = jnp.mean(kblk.astype(F32), axis=2)
    kblk_h = kblk.transpose(0, 3, 1, 2, 4)
    vblk_h = vblk.transpose(0, 3, 1, 2, 4)
    n_sel = min(MOBA_TOPK, n_blk - 1)
    scale = HEAD_DIM ** -0.5
    bi = jnp.arange(Bn)[:, None, None, None]
    hi = jnp.arange(N_HEADS)[None, :, None, None]

    def one(i):
        q0 = i * Q_BLOCK
        qi = lax.dynamic_slice_in_dim(q, q0, Q_BLOCK, axis=1)
        q_pos = q0 + jnp.arange(Q_BLOCK)
        own = q0 // MOBA_BLOCK
        own_start = own * MOBA_BLOCK
        k_own = lax.dynamic_slice_in_dim(kp, own_start, MOBA_BLOCK, axis=1)
        v_own = lax.dynamic_slice_in_dim(vp, own_start, MOBA_BLOCK, axis=1)
        s_own = jnp.einsum('bqhd,bshd->bhqs', qi, k_own).astype(F32) * scale
        own_mask = (own_start + jnp.arange(MOBA_BLOCK))[None, :] <= q_pos[:, None]
        s_own = jnp.where(own_mask, s_own, -jnp.inf)
        if n_sel > 0:
            gate = jnp.einsum('bqhd,bnhd->bhqn', qi.astype(F32), k_mean)
            gate = jnp.where(jnp.arange(n_blk) < own, gate, -jnp.inf)
            _, sel = lax.top_k(gate, n_sel)
            sel_ok = sel < own
            k_sel = kblk_h[bi, hi, sel]
            v_sel = vblk_h[bi, hi, sel]
            s_sel = jnp.einsum('bqhd,bhqjsd->bhqjs', qi, k_sel).astype(F32) * scale
            s_sel = jnp.where(sel_ok[..., None], s_sel, -jnp.inf)
            s_sel = s_sel.reshape(Bn, N_HEADS, Q_BLOCK, n_sel * MOBA_BLOCK)
            pr = jax.nn.softmax(jnp.concatenate([s_sel, s_own], -1), axis=-1)
            p_sel = pr[..., :n_sel * MOBA_BLOCK].reshape(Bn, N_HEADS, Q_BLOCK, n_sel, MOBA_BLOCK)
            p_own = pr[..., n_sel * MOBA_BLOCK:]
            return (jnp.einsum('bhqjs,bhqjsd->bqhd', p_sel.astype(v.dtype), v_sel)
                    + jnp.einsum('bhqs,bshd->bqhd', p_own.astype(v.dtype), v_own))
        pr = jax.nn.softmax(s_own, axis=-1)
        return jnp.einsum('bhqs,bshd->bqhd', pr.astype(v.dtype), v_own)

    out = lax.map(one, jnp.arange(S // Q_BLOCK))
    return out.transpose(1, 0, 2, 3, 4).reshape(Bn, S, N_HEADS * HEAD_DIM)


def moba_sample(q, k, v, cache_k_l, cache_v_l, page_table):
    Bn, T = q.shape[:2]
    kp = cache_k_l[page_table].reshape(Bn, PAST_LEN, N_HEADS, HEAD_DIM)
    vp = cache_v_l[page_table].reshape(Bn, PAST_LEN, N_HEADS, HEAD_DIM)
    n_pf = PAST_LEN // MOBA_BLOCK
    own_start = n_pf * MOBA_BLOCK
    tail = PAST_LEN - own_start
    k_own = jnp.concatenate([kp[:, own_start:], k], axis=1)
    v_own = jnp.concatenate([vp[:, own_start:], v], axis=1)
    scale = HEAD_DIM ** -0.5
    s_own = jnp.einsum('bqhd,bshd->bhqs', q, k_own).astype(F32) * scale
    own_mask = jnp.concatenate([jnp.ones((T, tail), bool), jnp.tril(jnp.ones((T, T), bool))], axis=1)
    s_own = jnp.where(own_mask, s_own, -jnp.inf)
    n_sel = min(MOBA_TOPK, n_pf)
    if n_sel > 0:
        kblk = kp[:, :own_start].reshape(Bn, n_pf, MOBA_BLOCK, N_HEADS, HEAD_DIM)
        vblk = vp[:, :own_start].reshape(Bn, n_pf, MOBA_BLOCK, N_HEADS, HEAD_DIM)
        k_mean = jnp.mean(kblk.astype(F32), axis=2)
        gate = jnp.einsum('bqhd,bnhd->bhqn', q.astype(F32), k_mean)
        _, sel = lax.top_k(gate, n_sel)
        bi = jnp.arange(Bn)[:, None, None, None]
        hi = jnp.arange(N_HEADS)[None, :, None, None]
        k_sel = kblk.transpose(0, 3, 1, 2, 4)[bi, hi, sel]
        v_sel = vblk.transpose(0, 3, 1, 2, 4)[bi, hi, sel]
        s_sel = jnp.einsum('bqhd,bhqjsd->bhqjs', q, k_sel).astype(F32) * scale
        s_sel = s_sel.reshape(Bn, N_HEADS, T, n_sel * MOBA_BLOCK)
        pr = jax.nn.softmax(jnp.concatenate([s_sel, s_own], -1), axis=-1)
        p_sel = pr[..., :n_sel * MOBA_BLOCK].reshape(Bn, N_HEADS, T, n_sel, MOBA_BLOCK)
        p_own = pr[..., n_sel * MOBA_BLOCK:]
        o = (jnp.einsum('bhqjs,bhqjsd->bqhd', p_sel.astype(v.dtype), v_sel)
             + jnp.einsum('bhqs,bshd->bqhd', p_own.astype(v.dtype), v_own))
    else:
        pr = jax.nn.softmax(s_own, axis=-1)
        o = jnp.einsum('bhqs,bshd->bqhd', pr.astype(v.dtype), v_own)
    return o.reshape(Bn, T, N_HEADS * HEAD_DIM)


def short_conv_mix(p, conv_prev, conv_w):
    G = GROUP_W
    T = p.shape[1]
    bg, cg, hv = p[..., :G], p[..., G:2 * G], p[..., 2 * G:]
    padded = jnp.concatenate([conv_prev.astype(p.dtype), cg * hv], axis=1)
    y = sum(conv_w[j] * padded[:, j:j + T] for j in range(CONV_W))
    return bg * y, padded[:, T:]


def chunk_mlp_mix(p, ln_g, ln_b, ws, bs):
    Bn, T, _ = p.shape
    G = GROUP_W
    u = jax.nn.gelu(p[..., :G])
    v = layer_norm(jax.nn.gelu(p[..., G:]), ln_g, ln_b)
    n_ch = -(-T // CHUNK)
    vp = jnp.pad(v, ((0, 0), (0, n_ch * CHUNK - T), (0, 0))).reshape(Bn, n_ch, CHUNK, N_HEADS, HEAD_DIM)
    wm = ws * jnp.tril(jnp.ones((CHUNK, CHUNK), ws.dtype))
    mixed = jnp.einsum('hij,bcjhd->bcihd', wm, vp) + bs.T[None, None, :, :, None]
    mixed = mixed.reshape(Bn, n_ch * CHUNK, G)[:, :T]
    return u * mixed, v


def peer_ffn(h, wq, keys, U, V):
    Bn, T, D = h.shape
    n = Bn * T
    blk = min(PEER_TOKEN_BLOCK, n)
    n_blocks = -(-n // blk)
    flat = jnp.pad(h.reshape(n, D), ((0, n_blocks * blk - n), (0, 0)))

    def one(xb):
        q = (xb @ wq).reshape(blk, PEER_HEADS, 2, PEER_HALF)
        s = jnp.einsum('nhcd,hckd->nhck', q, keys).astype(F32)
        v1, i1 = lax.top_k(s[:, :, 0], PEER_TOPK)
        v2, i2 = lax.top_k(s[:, :, 1], PEER_TOPK)
        cand = (v1[..., :, None] + v2[..., None, :]).reshape(blk, PEER_HEADS, PEER_TOPK * PEER_TOPK)
        cidx = (i1[..., :, None] * PEER_KEYS + i2[..., None, :]).reshape(blk, PEER_HEADS, PEER_TOPK * PEER_TOPK)
        best, pos = lax.top_k(cand, PEER_TOPK)
        eidx = jnp.take_along_axis(cidx, pos, axis=-1)
        gates = jax.nn.softmax(best, axis=-1)
        act = jax.nn.gelu(jnp.einsum('nhkd,nd->nhk', U[eidx], xb).astype(F32))
        return jnp.einsum('nhk,nhkd->nd', (gates * act).astype(xb.dtype), V[eidx])

    out = lax.map(one, flat.reshape(n_blocks, blk, D))
    return out.reshape(n_blocks * blk, D)[:n].reshape(Bn, T, D)


def forward_group(x, c, rw_s0, shift0, conv0, attend, W):
    G = GROUP_W
    Bn, T, _ = x.shape
    x = layer_norm(x, W['ln_in_g'], W['ln_in_b'])
    ks, vs, rws, shs, cvs, cms = [], [], [], [], [], []
    o_at = RW_COLS
    o_cv = o_at + MOBA_COLS
    o_cm = o_cv + CONV_COLS
    for l in range(DEPTH):
        m = jax.nn.silu(c) @ W['w_ada'][l] + W['b_ada'][l]
        sh1, sc1, g1, sh2, sc2, g2 = jnp.split(m[:, None, :], 6, axis=-1)
        pj = (x * (1 + sc1) + sh1) @ W['w_mix'][l]
        y_rw, s_rw, sh_rw = rwkv7_mix(pj[..., :RW_COLS], shift0[l], rw_s0[l], W['rw_mu'][l],
                                      W['rw_w0'][l], W['rw_w2'][l], W['rw_a0'][l], W['rw_a2'][l],
                                      W['rw_g2'][l], W['rw_kk'][l], W['rw_ka'][l], W['rw_rk'][l],
                                      W['rw_lnx_g'][l], W['rw_lnx_b'][l])
        q = pj[..., o_at:o_at + G].reshape(Bn, T, N_HEADS, HEAD_DIM)
        k = pj[..., o_at + G:o_at + 2 * G].reshape(Bn, T, N_HEADS, HEAD_DIM)
        v = pj[..., o_at + 2 * G:o_cv].reshape(Bn, T, N_HEADS, HEAD_DIM)
        y_at = attend(l, q, k, v)
        y_cv, cv_st = short_conv_mix(pj[..., o_cv:o_cm], conv0[l], W['conv_w'][l])
        y_cm, v_rows = chunk_mlp_mix(pj[..., o_cm:], W['cm_ln_g'][l], W['cm_ln_b'][l],
                                     W['cm_ws'][l], W['cm_bs'][l])
        mix = jnp.concatenate([y_rw, y_at, y_cv, y_cm], axis=-1)
        x = layer_norm(ALPHA * x + (1 + g1) * (mix @ W['w_out'][l]), W['ln1_g'][l], W['ln1_b'][l])
        f = peer_ffn(x * (1 + sc2) + sh2, W['peer_wq'][l], W['peer_keys'][l],
                     W['peer_u'][l], W['peer_v'][l])
        x = layer_norm(ALPHA * x + (1 + g2) * f, W['ln2_g'][l], W['ln2_b'][l])
        ks.append(k)
        vs.append(v)
        rws.append(s_rw)
        shs.append(sh_rw)
        cvs.append(cv_st)
        cms.append(v_rows)
    return (x, jnp.stack(ks), jnp.stack(vs), jnp.stack(rws), jnp.stack(shs),
            jnp.stack(cvs), jnp.stack(cms))


def setup_inputs(seed: int = 0) -> dict:
    key = jax.random.key(seed)
    k = jax.random.split(key, 48)

    def nrm(i, shape, s):
        return jax.random.normal(k[i], shape, jnp.float32) * s

    G = GROUP_W
    n_pages = PAST_LEN // PAGE_SIZE
    n_used = DEC_BATCH * n_pages
    n_pool = n_used + max(1, n_used // 4)
    page_table = jax.random.permutation(k[0], n_pool)[:n_used].reshape(DEC_BATCH, n_pages).astype(jnp.int32)
    return {
        'x_prompt': nrm(1, (BATCH, SEQ, D_MODEL), 1.0),
        'x_sample': nrm(2, (DEC_BATCH, DEC_SEQ, D_MODEL), 1.0),
        'cache_k': nrm(3, (DEPTH, n_pool, PAGE_SIZE, N_HEADS, HEAD_DIM), 1.0),
        'cache_v': nrm(4, (DEPTH, n_pool, PAGE_SIZE, N_HEADS, HEAD_DIM), 1.0),
        'state_rwkv': nrm(5, (DEPTH, DEC_BATCH, N_HEADS, HEAD_DIM, HEAD_DIM), 0.3),
        'state_shift': nrm(6, (DEPTH, DEC_BATCH, RW_COLS), 1.0),
        'state_conv': nrm(7, (DEPTH, DEC_BATCH, CONV_W - 1, G), 1.0),
        'page_table': page_table,
        'c_prompt': nrm(8, (BATCH, D_MODEL), 1.0),
        'c_sample': nrm(9, (DEC_BATCH, D_MODEL), 1.0),
        'ln_in_g': 1.0 + nrm(10, (D_MODEL,), 0.02),
        'ln_in_b': nrm(11, (D_MODEL,), 0.02),
        'w_ada': nrm(12, (DEPTH, D_MODEL, 6 * D_MODEL), 0.1 * D_MODEL ** -0.5),
        'b_ada': nrm(13, (DEPTH, 6 * D_MODEL), 0.01),
        'w_mix': nrm(14, (DEPTH, D_MODEL, N_COLS), D_MODEL ** -0.5),
        'rw_mu': jax.random.uniform(k[15], (DEPTH, RW_COLS), jnp.float32),
        'rw_w0': nrm(16, (DEPTH, G), 0.5),
        'rw_w2': nrm(17, (DEPTH, RW_W_RANK, G), 0.1 * RW_W_RANK ** -0.5),
        'rw_a0': nrm(18, (DEPTH, G), 0.1),
        'rw_a2': nrm(19, (DEPTH, RW_A_RANK, G), 0.1 * RW_A_RANK ** -0.5),
        'rw_g2': nrm(20, (DEPTH, RW_G_RANK, G), RW_G_RANK ** -0.5),
        'rw_kk': 0.85 + nrm(21, (DEPTH, G), 0.05),
        'rw_ka': 1.0 + nrm(22, (DEPTH, G), 0.05),
        'rw_rk': nrm(23, (DEPTH, N_HEADS, HEAD_DIM), 0.1),
        'rw_lnx_g': 1.0 + nrm(24, (DEPTH, G), 0.02),
        'rw_lnx_b': nrm(25, (DEPTH, G), 0.02),
        'conv_w': nrm(26, (DEPTH, CONV_W, G), CONV_W ** -0.5),
        'cm_ln_g': 1.0 + nrm(27, (DEPTH, G), 0.02),
        'cm_ln_b': nrm(28, (DEPTH, G), 0.02),
        'cm_ws': nrm(29, (DEPTH, N_HEADS, CHUNK, CHUNK), CHUNK ** -0.5),
        'cm_bs': 1.0 + nrm(30, (DEPTH, N_HEADS, CHUNK), 0.1),
        'w_out': nrm(31, (DEPTH, D_MIX, D_MODEL), BETA * D_MIX ** -0.5),
        'ln1_g': 1.0 + nrm(32, (DEPTH, D_MODEL), 0.02),
        'ln1_b': nrm(33, (DEPTH, D_MODEL), 0.02),
        'ln2_g': 1.0 + nrm(34, (DEPTH, D_MODEL), 0.02),
        'ln2_b': nrm(35, (DEPTH, D_MODEL), 0.02),
        'peer_wq': nrm(36, (DEPTH, D_MODEL, PEER_HEADS * PEER_QDIM), D_MODEL ** -0.5),
        'peer_keys': nrm(37, (DEPTH, PEER_HEADS, 2, PEER_KEYS, PEER_HALF), PEER_HALF ** -0.5),
        'peer_u': nrm(38, (DEPTH, PEER_EXPERTS, D_MODEL), D_MODEL ** -0.5),
        'peer_v': nrm(39, (DEPTH, PEER_EXPERTS, D_MODEL), BETA * PEER_HEADS ** -0.5),
    }


def reference(x_prompt, x_sample, cache_k, cache_v, state_rwkv, state_shift, state_conv, page_table,
              c_prompt, c_sample, ln_in_g, ln_in_b, w_ada, b_ada, w_mix, rw_mu, rw_w0, rw_w2, rw_a0,
              rw_a2, rw_g2, rw_kk, rw_ka, rw_rk, rw_lnx_g, rw_lnx_b, conv_w, cm_ln_g, cm_ln_b, cm_ws,
              cm_bs, w_out, ln1_g, ln1_b, ln2_g, ln2_b, peer_wq, peer_keys, peer_u, peer_v):
    W = {'ln_in_g': ln_in_g, 'ln_in_b': ln_in_b, 'w_ada': w_ada, 'b_ada': b_ada, 'w_mix': w_mix,
         'rw_mu': rw_mu, 'rw_w0': rw_w0, 'rw_w2': rw_w2, 'rw_a0': rw_a0, 'rw_a2': rw_a2,
         'rw_g2': rw_g2, 'rw_kk': rw_kk, 'rw_ka': rw_ka, 'rw_rk': rw_rk, 'rw_lnx_g': rw_lnx_g,
         'rw_lnx_b': rw_lnx_b, 'conv_w': conv_w, 'cm_ln_g': cm_ln_g, 'cm_ln_b': cm_ln_b,
         'cm_ws': cm_ws, 'cm_bs': cm_bs, 'w_out': w_out, 'ln1_g': ln1_g, 'ln1_b': ln1_b,
         'ln2_g': ln2_g, 'ln2_b': ln2_b, 'peer_wq': peer_wq, 'peer_keys': peer_keys,
         'peer_u': peer_u, 'peer_v': peer_v}
    Bp = x_prompt.shape[0]
    dt = x_prompt.dtype
    z_rw = jnp.zeros((DEPTH, Bp, N_HEADS, HEAD_DIM, HEAD_DIM), dt)
    z_sh = jnp.zeros((DEPTH, Bp, RW_COLS), dt)
    z_cv = jnp.zeros((DEPTH, Bp, CONV_W - 1, GROUP_W), dt)
    y_p, k_p, v_p, rw_p, sh_p, cv_p, _ = forward_group(
        x_prompt, c_prompt, z_rw, z_sh, z_cv,
        lambda l, q, k, v: moba_prompt(q, k, v), W)
    y_s, k_s, v_s, rw_s, sh_s, cv_s, cm_s = forward_group(
        x_sample, c_sample, state_rwkv, state_shift, state_conv,
        lambda l, q, k, v: moba_sample(q, k, v, cache_k[l], cache_v[l], page_table), W)
    return (y_p, y_s, k_p, v_p, k_s, v_s, rw_p, rw_s, sh_p, sh_s, cv_p, cv_s, cm_s)
```

```python
from contextlib import ExitStack
import numpy as np
import concourse.bass as bass
import concourse.mybir as mybir
from concourse.bass_utils import run_bass_kernel_spmd

F32 = mybir.dt.float32
BF16 = mybir.dt.bfloat16
I32 = mybir.dt.int32
AF = mybir.ActivationFunctionType
ALU = mybir.AluOpType
AX = mybir.AxisListType

ENGS = ("pe", "act", "dve", "pool", "sp")


class Prog:
    def __init__(self, nc, stack):
        self.nc = nc
        self.stack = stack
        self.streams = {e: [] for e in ENGS}
        self.sem = {}
        self.cnt = {}
        self.seen = {e: {} for e in ENGS}
        self.res = {}
        for e in ENGS:
            self._mksem(e)
        self.n_ops = 0

    def _mksem(self, name):
        if name not in self.sem:
            self.sem[name] = self.stack.enter_context(self.nc.semaphore("s_" + name))
            self.cnt[name] = 0
        return name

    def _deps(self, r, w):
        deps = {}

        def add(sc):
            if sc is None:
                return
            s, c = sc
            if deps.get(s, 0) < c:
                deps[s] = c
        for k in r:
            ent = self.res.get(k)
            if ent:
                add(ent[0])
        for k in w:
            ent = self.res.get(k)
            if ent:
                add(ent[0])
                for s, c in ent[1].items():
                    add((s, c))
        return deps

    def _commit(self, r, w, me):
        for k in r:
            ent = self.res.setdefault(k, [None, {}])
            ent[1][me[0]] = me[1]
        for k in w:
            self.res[k] = [me, {}]

    def op(self, eng, fn, r=(), w=()):
        deps = self._deps(r, w)
        waits = []
        for s, c in deps.items():
            if s == "pe" and eng == "pe":
                continue
            if self.seen[eng].get(s, 0) < c:
                waits.append((s, c))
                self.seen[eng][s] = c
        self.cnt[eng] += 1
        me = (eng, self.cnt[eng])
        self._commit(r, w, me)
        self.streams[eng].append((waits, fn, eng, 1))
        self.n_ops += 1

    def dma(self, q, fn, r=(), w=(), sem=None):
        self._mksem(sem)
        deps = self._deps(r, w)
        waits = []
        for s, c in deps.items():
            if self.seen[q].get(s, 0) < c:
                waits.append((s, c))
                self.seen[q][s] = c
        if self.cnt[sem] > 0 and self.seen[q].get(sem, 0) < self.cnt[sem]:
            waits.append((sem, self.cnt[sem]))
            self.seen[q][sem] = self.cnt[sem]
        self.cnt[sem] += 16
        me = (sem, self.cnt[sem])
        self._commit(r, w, me)
        self.streams[q].append((waits, fn, sem, 16))
        self.n_ops += 1

    def wait_all(self, eng):
        waits = []
        for s, c in self.cnt.items():
            if c > 0 and self.seen[eng].get(s, 0) < c and s != eng:
                waits.append((s, c))
                self.seen[eng][s] = c
        self.streams[eng].append((waits, None, None, 0))

    def emit(self):
        nc = self.nc
        hmap = {"pe": "tensor", "act": "scalar", "dve": "vector", "pool": "gpsimd", "sp": "sync"}
        with nc.Block() as block:
            for e in ENGS:
                stream = self.streams[e]

                def body(h, stream=stream):
                    for waits, fn, s, inc in stream:
                        for ws, wc in waits:
                            h.wait_ge(self.sem[ws], wc)
                        if fn is not None:
                            ins = fn(h)
                            ins.then_inc(self.sem[s], inc)
                getattr(block, hmap[e])(body)


D = 1024
G = 256
HD = 64
NH = 4
RW_COLS = 896
N_COLS = 2944
O_AT = 896
O_CV = O_AT + 768
O_CM = O_CV + 768
NKEY = 128
PHEADS = 8
PTOPK = 16
LN_EPS = 1e-5
GN_EPS = 64e-5
BIG = 30000.0


def default_cfg():
    return dict(L=4, T=4096, NS=4, TS=8, GN=256, PAST=8192, POOL=2560, NSEL=3,
                stages=("conv", "cmlp", "rwkv", "moba", "peer", "mobas"))


def prm_layout(L):
    off = {}
    n = 0

    def add(name, k):
        nonlocal n
        off[name] = n
        n += k
    add("ln_in_g", 8)
    add("ln_in_b", 8)
    for l in range(L):
        for nm, k in (("b_ada", 48), ("ln1_g", 8), ("ln1_b", 8), ("ln2_g", 8), ("ln2_b", 8), ("rw_mu", 7),
                      ("rw_w0", 2), ("rw_a0", 2), ("rw_kk", 2), ("rw_ka", 2), ("rw_rk", 2), ("lnx_g", 2),
                      ("lnx_b", 2), ("conv_w", 6), ("cm_ln_g", 2), ("cm_ln_b", 2)):
            add((nm, l), k)
    return off, n


def const_table():
    cols = {}
    parts = []
    n = 0

    def add(name, a):
        nonlocal n
        a = np.asarray(a, np.float32).reshape(128, -1)
        cols[name] = (n, a.shape[1])
        parts.append(a)
        n += a.shape[1]
    add("ident", np.eye(128))
    add("ones", np.ones((128, 128)))
    bo = np.zeros((128, 128))
    bo[:64, :64] = 1
    bo[64:, 64:] = 1
    add("blockones", bo)
    i2 = np.zeros((128, 64))
    i2[np.arange(128), np.arange(128) % 64] = 1
    add("i2", i2)
    add("tri", np.triu(np.ones((128, 128))))
    ho = np.zeros((128, 2, 128))
    ho[:, 0, :64] = 1
    ho[:, 1, 64:] = 1
    add("halfones", ho)
    return np.concatenate(parts, axis=1), cols


def build(cfg):
    L, T, NS, TS, GN = cfg["L"], cfg["T"], cfg["NS"], cfg["TS"], cfg["GN"]
    PAST, POOL, NSEL = cfg["PAST"], cfg["POOL"], cfg["NSEL"]
    stages = cfg["stages"]
    NSQ = 1 + NS
    TSA = NS * TS
    TT = T + TSA
    NPG = T // GN
    NPAGE = PAST // 128
    NBLK_S = PAST // 256
    NTILE = T // 128
    ALPHA = (2.0 * L) ** 0.25
    poff, NPRM = prm_layout(L)
    ctab, ccols = const_table()
    NCONST = ctab.shape[1]

    nc = bass.Bass("TRN2", target_bir_lowering=False)

    def din(name, shape, dt=F32):
        return nc.dram_tensor(name, list(shape), dt, kind="ExternalInput").ap()

    def dout(name, shape, dt=F32):
        return nc.dram_tensor(name, list(shape), dt, kind="ExternalOutput").ap()

    xT_in = din("xT", [8, 128, TT])
    cT_in = din("cT", [128, 8, NSQ])
    prm_in = din("prm", [128, NPRM])
    cst_in = din("cst", [128, NCONST])
    wada_in = din("w_ada", [L, 128, 8, 6144])
    wmix_in = din("w_mix", [L, 128, 8, N_COLS])
    wout_in = din("w_out", [L, 128, 8, D])
    wq_in = din("wq", [L, 128, 8, 2048])
    lrw_in = din("lrw", [L, 128, 256])
    keysT_in = din("keysT", [L, 128, 16, 128])
    cmws_in = din("cmwsT", [L, 128, 4, 128])
    cmbs_in = din("cmbs", [L, 1, 512])
    UT_in = din("UT", [L, 128, 8, NKEY * NKEY])
    V_in = din("PV", [L, NKEY * NKEY, D])
    ck_in = din("cache_k", [L, POOL, 128, 256])
    cv_in = din("cache_v", [L, POOL, 128, 256])
    pt_in = din("ptab", [1, NS * NPAGE], I32)
    strw_in = din("st_rw", [L, NS, 2, 128, 64])
    stsh_in = din("st_sh", [L, NS, 128, 7])
    stcv_in = din("st_cv", [L, NS, 128, 4])

    yT_out = dout("yT", [8, 128, TT])
    kT_out = dout("kT", [L, 2, 128, TT])
    vT_out = dout("vT", [L, 2, 128, TT])
    rw_out = dout("rw", [L, NSQ, 2, 128, 64])
    sh_out = dout("sh", [L, NSQ, 128, 7])
    cv_out = dout("cv", [L, NSQ, 128, 4])
    cm_out = dout("cm", [L, 2, 128, TSA])
    dbg_out = dout("dbg", [128, 48, NSQ])
    dbg2_out = dout("dbg2", [128, 8, GN])
    dbg3_out = dout("dbg3", [128, 8, TSA])
    xres = nc.dram_tensor("xres", [2, 8, 128, TT], F32, kind="Internal").ap()

    st = ExitStack()
    P = Prog(nc, st)

    def sb(name, shape, dt=F32):
        return st.enter_context(nc.sbuf_tensor(name, list(shape), dt))

    def psb(name, shape, dt=F32):
        return st.enter_context(nc.psum_tensor(name, list(shape), dt))

    CST = sb("CST", [128, NCONST])
    PRM = sb("PRM", [128, NPRM])
    CTs = sb("CTs", [128, 8, NSQ])
    MT = sb("MT", [128, 48, NSQ])
    MT1 = sb("MT1", [128, 48, NSQ])
    X = sb("X", [128, 8, GN])
    XM = sb("XM", [128, 8, GN])
    R1 = sb("R1", [128, 8, GN])
    SQ = sb("SQ", [128, 8, GN])
    PJ = sb("PJ", [128, 7, GN])
    XR = sb("XR", [128, 7, GN])
    MIX = sb("MIX", [128, 8, GN])
    LNA = sb("LNA", [128, GN])
    LNB = sb("LNB", [128, GN])
    LNC = sb("LNC", [128, GN])
    WR = [sb(f"WR{i}", [128, 8, 128]) for i in range(3)]
    WRA = [sb(f"WRA{i}", [128, 8, 128]) for i in range(2)]
    wr_i = [0]
    ZCOL = sb("ZCOL", [128, 8])
    IDX = sb("IDX", [128, NS * NPAGE], I32)
    IOT = sb("IOT", [128, 1], I32)
    ONE = sb("ONE", [128, 2])

    S_rw = sb("S_rw", [128, NSQ, 2, 64])
    SHIFT = sb("SHIFT", [128, NSQ, 7])
    CONV = sb("CONV", [128, NSQ, 2, 2])

    pA = psb("pA", [128, 512])
    pB = psb("pB", [128, 512])
    pL1 = psb("pL1", [128, 512])
    pL2 = psb("pL2", [128, 512])
    pC = psb("pC", [128, 512])
    pD = psb("pD", [128, 512])
    pE = psb("pE", [128, 512])
    pF = psb("pF", [128, 512])

    def cs(name, w=None):
        o, n = ccols[name]
        return CST[:, o:o + n] if w is None else CST[:, o:o + w]

    ident = cs("ident")
    ones = cs("ones")
    blockones = cs("blockones")

    def pc(name, l=None, k=0):
        o = poff[name if l is None else (name, l)] + k
        return PRM[:, o:o + 1]

    P.dma("sp", lambda e: e.dma_start(out=CST[:], in_=cst_in), w=["CST"], sem="d_c0")
    P.dma("sp", lambda e: e.dma_start(out=PRM[:], in_=prm_in), w=["PRM"], sem="d_c1")
    P.dma("sp", lambda e: e.dma_start(out=CTs[:], in_=cT_in), w=["CTs"], sem="d_c2")
    P.dma("sp", lambda e: e.dma_start(out=IDX[:], in_=pt_in.partition_broadcast(128)), w=["IDX"], sem="d_c3")
    P.op("pool", lambda e: e.iota(IOT[:], pattern=[[0, 1]], base=0, channel_multiplier=1), w=["IOT"])
    P.op("dve", lambda e: e.tensor_scalar(out=IDX[:], in0=IDX[:], scalar1=128, scalar2=IOT[:, 0:1], op0=ALU.mult, op1=ALU.add),
         r=["IDX", "IOT"], w=["IDX"])
    ckr = ck_in.rearrange("l n t f -> (l n t) f")
    cvr = cv_in.rearrange("l n t f -> (l n t) f")
    P.op("act", lambda e: e.activation(out=CTs[:], in_=CTs[:], func=AF.Silu), r=["CTs"], w=["CTs"])
    P.op("dve", lambda e: e.memset(ZCOL[:], 0.0), w=["ZCOL"])
    P.op("dve", lambda e: e.memset(ONE[:, 0:1], 1.0), w=["ONE"])
    P.op("dve", lambda e: e.memset(ONE[:, 1:2], -0.5), w=["ONE"])

    groups = []
    for g in range(NPG):
        groups.append(dict(N=GN, segs=[(0, 0, GN, g * GN)], gcol=g * GN, first=(g == 0), last=(g == NPG - 1), g=g))
    groups.append(dict(N=TSA, segs=[(1 + b, b * TS, TS, T + b * TS) for b in range(NS)], gcol=T, first=True, last=True, g=-1))

    def ring():
        i = wr_i[0]
        wr_i[0] = (i + 1) % 3
        return i

    def proj(l, w_in, col0, ncols_chunks, src, src_key, N, sink):
        for j in range(ncols_chunks):
            s = ring()
            c0 = col0 + j * 128
            P.dma("sp", lambda e, s=s, c0=c0: e.dma_start(out=WR[s][:], in_=w_in[l, :, :, c0:c0 + 128]),
                  w=[("WR", s)], sem=f"d_wr{s}")
            pt = pA if (j % 2 == 0) else pB
            pk = "pA" if (j % 2 == 0) else "pB"

            def mm(e, s=s, pt=pt):
                for k in range(8):
                    i = e.matmul(pt[:, :N], lhsT=WR[s][:, k, :], rhs=src[:, k, :N], start=(k == 0), stop=(k == 7))
                return i
            P.op("pe", mm, r=[("WR", s), src_key], w=[pk])
            sink(j, pt, pk)

    def ln_fm(src, src_key, dst, dst_key, N, gname, bname, l, nch=8, s0=0, d0=0, eps=LN_EPS):
        dim = nch * 128

        def m1(e):
            for k in range(nch):
                i = e.matmul(pL1[:, :N], lhsT=ones, rhs=src[:, s0 + k, :N], start=(k == 0), stop=(k == nch - 1))
            return i
        P.op("pe", m1, r=[src_key, "CST"], w=["pL1"])
        P.op("act", lambda e: e.activation(out=SQ[:, :nch, :N], in_=src[:, s0:s0 + nch, :N], func=AF.Square), r=[src_key], w=["SQ"])

        def m2(e):
            for k in range(nch):
                i = e.matmul(pL2[:, :N], lhsT=ones, rhs=SQ[:, k, :N], start=(k == 0), stop=(k == nch - 1))
            return i
        P.op("pe", m2, r=["SQ", "CST"], w=["pL2"])
        P.op("act", lambda e: e.mul(LNA[:, :N], pL1[:, :N], 1.0 / dim), r=["pL1"], w=["LNA"])
        P.op("dve", lambda e: e.tensor_tensor(out=LNB[:, :N], in0=LNA[:, :N], in1=LNA[:, :N], op=ALU.mult), r=["LNA"], w=["LNB"])
        P.op("dve", lambda e: e.scalar_tensor_tensor(out=LNB[:, :N], in0=pL2[:, :N], scalar=1.0 / dim, in1=LNB[:, :N],
                                                     op0=ALU.mult, op1=ALU.subtract), r=["pL2", "LNB"], w=["LNB"])
        P.op("dve", lambda e: e.tensor_scalar(out=LNB[:, :N], in0=LNB[:, :N], scalar1=0.0, scalar2=eps, op0=ALU.max, op1=ALU.add),
             r=["LNB"], w=["LNB"])
        P.op("act", lambda e: e.activation(out=LNC[:, :N], in_=LNB[:, :N], func=AF.Sqrt), r=["LNB"], w=["LNC"])
        P.op("dve", lambda e: e.reciprocal(LNC[:, :N], LNC[:, :N]), r=["LNC"], w=["LNC"])
        for k in range(nch):
            P.op("dve", lambda e, k=k: e.tensor_tensor(out=SQ[:, k, :N], in0=src[:, s0 + k, :N], in1=LNA[:, :N], op=ALU.subtract),
                 r=[src_key, "LNA"], w=["SQ"])
            P.op("dve", lambda e, k=k: e.tensor_tensor(out=SQ[:, k, :N], in0=SQ[:, k, :N], in1=LNC[:, :N], op=ALU.mult),
                 r=["SQ", "LNC"], w=["SQ"])
            P.op("act", lambda e, k=k: e.activation(out=dst[:, d0 + k, :N], in_=SQ[:, k, :N], func=AF.Identity,
                                                    scale=pc(gname, l, k), bias=pc(bname, l, k)),
                 r=["SQ", "PRM"], w=[dst_key])

    def gelu_tanh(dst, src, keys_r, keys_w, tmp):
        P.op("dve", lambda e: e.tensor_tensor(out=tmp, in0=src, in1=src, op=ALU.mult), r=keys_r, w=["GT"])
        P.op("dve", lambda e: e.tensor_scalar(out=tmp, in0=tmp, scalar1=0.044715, scalar2=1.0, op0=ALU.mult, op1=ALU.add), r=["GT"], w=["GT"])
        P.op("dve", lambda e: e.tensor_tensor(out=tmp, in0=tmp, in1=src, op=ALU.mult), r=keys_r + ["GT"], w=["GT"])
        P.op("act", lambda e: e.activation(out=tmp, in_=tmp, func=AF.Tanh, scale=0.7978845608028654), r=["GT"], w=["GT"])
        P.op("dve", lambda e: e.scalar_tensor_tensor(out=tmp, in0=tmp, scalar=1.0, in1=src, op0=ALU.add, op1=ALU.mult), r=keys_r + ["GT"], w=["GT"])
        P.op("act", lambda e: e.mul(dst, tmp, 0.5), r=["GT"], w=keys_w)

    def modulate(src, src_key, dst, dst_key, grp, sc_slot, sh_slot):
        for (sq, c0, n, _) in grp["segs"]:
            for k in range(8):
                P.op("act", lambda e, k=k, sq=sq, c0=c0, n=n: e.activation(
                    out=dst[:, k, c0:c0 + n], in_=src[:, k, c0:c0 + n], func=AF.Identity,
                    scale=MT1[:, sc_slot * 8 + k, sq:sq + 1], bias=MT[:, sh_slot * 8 + k, sq:sq + 1]),
                    r=[src_key, "MT"], w=[dst_key])

    for grp in groups:
        N = grp["N"]
        gc = grp["gcol"]
        P.dma("sp", lambda e, gc=gc, N=N: e.dma_start(out=X[:, :, :N], in_=xT_in[:, :, gc:gc + N].rearrange("c p t -> p c t")),
              w=["X"], sem="d_x")
        ln_fm(X, "X", XM, "XM", N, "ln_in_g", "ln_in_b", None)
        P.dma("sp", lambda e, gc=gc, N=N: e.dma_start(out=xres[0, :, :, gc:gc + N].rearrange("c p t -> p c t"), in_=XM[:, :, :N]),
              r=["XM"], w=[("xres", 0, gc)], sem="d_xs")

    LRW = sb("LRW", [128, 256])
    NW0 = sb("NW0", [128, 2])
    Q5 = sb("Q5", [128, 2, 5, GN])
    DG = sb("DG", [128, 5, 8, 64])
    REP = sb("REP", [128, 2, 5, 512])
    NK0 = sb("NK0", [128, 2, max(NS, 1)])
    Wt = sb("Wt", [128, 2, 6, 2, 3])
    TMPA = sb("TMPA", [128, 2, 2, 64])
    TMPB = sb("TMPB", [128, 2, 2, 64])
    DGv = DG[:].rearrange("p a b c -> p (a b c)").rearrange("p (c q t k) -> p c q t k", c=2, q=5, t=4)

    def Wyv(rb, t):
        return _wyv(Wt[:, rb].rearrange("p t c s -> p (t c s)"), t)

    def _wyv(flat, t):
        base = t * 6 + 2
        a0 = flat[:, base:base + 8]
        return a0.rearrange("p (j r) -> p j r", j=2)[:, :, 0:4:3].rearrange("p j c -> p c j")
    rep_i = [0]

    def rwkv_layer_setup(l):
        P.dma("sp", lambda e: e.dma_start(out=LRW[:], in_=lrw_in[l]), w=["LRW"], sem="d_lrw")
        o = poff[("rw_w0", l)]
        P.op("dve", lambda e: e.tensor_scalar(out=NW0[:], in0=PRM[:, o:o + 2], scalar1=-1.0, scalar2=None, op0=ALU.mult),
             r=["PRM"], w=["NW0"])

    def rwkv(l, grp):
        N = grp["N"]
        P.op("act", lambda e: e.activation(out=R1[0:32, 6, :N], in_=XR[0:32, 6, :N], func=AF.Tanh), r=["XR"], w=["R1"])
        P.op("act", lambda e: e.copy(R1[32:64, 6, :N], XR[32:64, 6, :N]), r=["XR"], w=["R1"])
        P.op("act", lambda e: e.activation(out=R1[64:128, 6, :N], in_=XR[64:128, 6, :N], func=AF.Sigmoid), r=["XR"], w=["R1"])
        for c in range(2):
            cc = slice(c * 128, (c + 1) * 128)
            P.op("pe", lambda e, cc=cc: e.matmul(pC[:, :N], lhsT=LRW[0:32, cc], rhs=R1[0:32, 6, :N], start=True, stop=True),
                 r=["LRW", "R1"], w=["pC"])
            P.op("act", lambda e, c=c: e.activation(out=Q5[:, c, 4, :N], in_=pC[:, :N], func=AF.Exp, scale=-1.0, bias=NW0[:, c:c + 1]),
                 r=["pC", "NW0"], w=["Q5"])
            P.op("act", lambda e, c=c: e.activation(out=Q5[:, c, 4, :N], in_=Q5[:, c, 4, :N], func=AF.Ln, bias=ONE[:, 0:1], scale=1.0),
                 r=["Q5"], w=["Q5"])
            P.op("act", lambda e, c=c: e.activation(out=Q5[:, c, 4, :N], in_=Q5[:, c, 4, :N], func=AF.Exp, scale=-1.0, bias=ONE[:, 1:2]),
                 r=["Q5"], w=["Q5"])
            P.op("act", lambda e, c=c: e.activation(out=Q5[:, c, 4, :N], in_=Q5[:, c, 4, :N], func=AF.Exp, scale=-1.0),
                 r=["Q5"], w=["Q5"])
            P.op("pe", lambda e, cc=cc: e.matmul(pD[:, :N], lhsT=LRW[32:64, cc], rhs=R1[32:64, 6, :N], start=True, stop=True),
                 r=["LRW", "R1"], w=["pD"])
            P.op("act", lambda e, c=c: e.activation(out=SQ[:, 2 + c, :N], in_=pD[:, :N], func=AF.Sigmoid, bias=pc("rw_a0", l, c), scale=1.0),
                 r=["pD", "PRM"], w=["SQ"])
            P.op("pe", lambda e, cc=cc: e.matmul(pC[:, :N], lhsT=LRW[64:128, cc], rhs=R1[64:128, 6, :N], start=True, stop=True),
                 r=["LRW", "R1"], w=["pC"])
            P.op("act", lambda e, c=c: e.copy(R1[:, 2 + c, :N], pC[:, :N]), r=["pC"], w=["R1"])
            P.op("dve", lambda e, c=c: e.tensor_scalar(out=R1[:, 7, :N], in0=XR[:, 2 + c, :N], scalar1=pc("rw_kk", l, c), scalar2=None, op0=ALU.mult),
                 r=["XR", "PRM"], w=["R1"])
            P.op("act", lambda e: e.activation(out=GT[:, 0, :N], in_=R1[:, 7, :N], func=AF.Square), r=["R1"], w=["GT"])
            P.op("pe", lambda e: e.matmul(pD[:, :N], lhsT=blockones, rhs=GT[:, 0, :N], start=True, stop=True), r=["GT", "CST"], w=["pD"])
            P.op("act", lambda e: e.activation(out=GT[:, 1, :N], in_=pD[:, :N], func=AF.Sqrt), r=["pD"], w=["GT"])
            P.op("dve", lambda e: e.tensor_scalar(out=GT[:, 1, :N], in0=GT[:, 1, :N], scalar1=1e-12, scalar2=None, op0=ALU.max), r=["GT"], w=["GT"])
            P.op("dve", lambda e: e.reciprocal(GT[:, 1, :N], GT[:, 1, :N]), r=["GT"], w=["GT"])
            P.op("dve", lambda e, c=c: e.scalar_tensor_tensor(out=GT[:, 0, :N], in0=R1[:, 7, :N], scalar=-1.0, in1=GT[:, 1, :N],
                                                             op0=ALU.mult, op1=ALU.mult), r=["R1", "GT"], w=["GT"])
            P.op("dve", lambda e, c=c: e.scalar_tensor_tensor(out=Q5[:, c, 2, :N], in0=GT[:, 0, :N], scalar=-1.0, in1=SQ[:, 2 + c, :N],
                                                             op0=ALU.mult, op1=ALU.mult), r=["GT", "SQ"], w=["Q5"])
            P.op("dve", lambda e, c=c: e.memset(Q5[:, c, 1, N - 1:N], 0.0), w=["Q5"])
            if N > 1:
                P.op("act", lambda e, c=c: e.copy(Q5[:, c, 1, 0:N - 1], GT[:, 0, 1:N]), r=["GT"], w=["Q5"])
            for si, (sq_, c0_, n_, _) in enumerate(grp["segs"]):
                P.op("act", lambda e, c=c, si=si, c0_=c0_: e.copy(NK0[:, c, si:si + 1], GT[:, 0, c0_:c0_ + 1]), r=["GT"], w=["NK0"])
            P.op("dve", lambda e, c=c: e.tensor_scalar(out=R1[:, 7, :N], in0=SQ[:, 2 + c, :N], scalar1=1.0, scalar2=pc("rw_ka", l, c),
                                                      op0=ALU.subtract, op1=ALU.mult), r=["SQ", "PRM", "R1"], w=["R1"])
            P.op("dve", lambda e, c=c: e.scalar_tensor_tensor(out=Q5[:, c, 3, :N], in0=R1[:, 7, :N], scalar=1.0, in1=XR[:, 2 + c, :N],
                                                             op0=ALU.add, op1=ALU.mult), r=["R1", "XR"], w=["Q5"])
            P.op("act", lambda e, c=c: e.copy(Q5[:, c, 0, :N], XR[:, c, :N]), r=["XR"], w=["Q5"])
        i2 = cs("i2")
        for si, (sq, c0, n, _) in enumerate(grp["segs"]):
            S = S_rw[:, sq]
            rb = rep_i[0]
            rep_i[0] = 1 - rb
            P.op("pool", lambda e, si=si: e.tensor_tensor(out=DGv[:, :, 0, 0, :], in0=NK0[:, :, si:si + 1].to_broadcast([128, 2, 64]),
                                                          in1=i2.unsqueeze(1).to_broadcast([128, 2, 64]), op=ALU.mult),
                 r=["NK0", "CST"], w=["DG"])
            P.op("pe", lambda e: e.matmul(pE[:, 0:128], lhsT=blockones, rhs=DGv[:, :, 0, 0, :], start=True, stop=True), r=["DG", "CST"], w=["pE"])
            P.op("act", lambda e: e.copy(TMPB[:, :, 0, :], pE[:, 0:128].rearrange("p (c k) -> p c k", c=2)), r=["pE"], w=["TMPB"])
            P.op("dve", lambda e, S=S: e.tensor_tensor(out=TMPA[:, :, 0, :], in0=S, in1=TMPB[:, :, 0, :], op=ALU.mult), r=["S_rw", "TMPB"], w=["TMPA"])
            P.op("dve", lambda e, rb=rb: e.tensor_reduce(out=Wt[:, rb, 0, :, 0], in_=TMPA[:, :, 0, :], axis=AX.X, op=ALU.add), r=["TMPA"], w=[("Wt", rb)])
            first = True
            for t0 in range(0, n, 4):
                nb = min(4, n - t0)
                col = c0 + t0
                if not first:
                    prb = rb
                    rb = rep_i[0]
                    rep_i[0] = 1 - rb
                    P.op("dve", lambda e, rb=rb, prb=prb, pnb=pnb: e.tensor_copy(Wt[:, rb, 0, :, 0], Wt[:, prb, pnb, :, 0]),
                         r=[("Wt", prb)], w=[("Wt", rb)])
                first = False
                pnb = nb
                Rv = REP[:, rb].rearrange("p q n -> p (q n)").rearrange("p (c q n) -> p c q n", c=2, q=5)
                for c in range(2):
                    P.op("pool", lambda e, c=c, col=col, nb=nb: e.tensor_tensor(
                        out=DGv[:, c, :, 0:nb, :], in0=Q5[:, c, :, col:col + nb].unsqueeze(3).to_broadcast([128, 5, nb, 64]),
                        in1=i2.unsqueeze(1).unsqueeze(1).to_broadcast([128, 5, nb, 64]), op=ALU.mult),
                        r=["Q5", "CST"], w=["DG"])
                for q in range(5):
                    pt, pk = (pE, "pE") if q % 2 == 0 else (pF, "pF")

                    def mmr(e, q=q, pt=pt, nb=nb):
                        for c in range(2):
                            i = e.matmul(pt[:, c * 256:c * 256 + nb * 64], lhsT=blockones,
                                         rhs=DGv[:, c, q, 0:nb, :].rearrange("p t k -> p (t k)"), start=True, stop=True)
                        return i
                    P.op("pe", mmr, r=["DG", "CST"], w=[pk])
                    P.op("act", lambda e, q=q, pt=pt, nb=nb, Rv=Rv: e.copy(Rv[:, :, q, 0:nb * 64],
                                                                          pt[:, :].rearrange("p (c n) -> p c n", c=2)[:, :, 0:nb * 64]),
                         r=[pk], w=[("REP", rb)])
                P.op("pool", lambda e, rb=rb, col=col, nb=nb: e.tensor_copy(Wt[:, rb, 0:nb, :, 1].rearrange("p t c -> p c t"), XR[:, 4:6, col:col + nb]),
                     r=["XR"], w=[("Wt", rb)])
                for t in range(nb):
                    ts_ = slice(t * 64, (t + 1) * 64)
                    P.op("dve", lambda e, rb=rb, t=t, Rv=Rv, ts_=ts_: e.tensor_tensor(
                        out=TMPB[:], in0=Rv[:, :, 2:4, ts_], in1=Wt[:, rb, t, :, 0:2].unsqueeze(3).to_broadcast([128, 2, 2, 64]), op=ALU.mult),
                        r=[("REP", rb), ("Wt", rb)], w=["TMPB"])
                    P.op("dve", lambda e, S=S, Rv=Rv, ts_=ts_: e.tensor_tensor(out=S, in0=S, in1=Rv[:, :, 4, ts_], op=ALU.mult),
                         r=["S_rw", ("REP", rb)], w=["S_rw"])
                    P.op("dve", lambda e, S=S: e.tensor_tensor(out=S, in0=S, in1=TMPB[:, :, 0, :], op=ALU.add), r=["S_rw", "TMPB"], w=["S_rw"])
                    P.op("dve", lambda e, S=S: e.tensor_tensor(out=S, in0=S, in1=TMPB[:, :, 1, :], op=ALU.add), r=["S_rw", "TMPB"], w=["S_rw"])
                    P.op("dve", lambda e, S=S, Rv=Rv, ts_=ts_: e.tensor_tensor(
                        out=TMPA[:], in0=S.unsqueeze(2).to_broadcast([128, 2, 2, 64]), in1=Rv[:, :, 0:2, ts_], op=ALU.mult),
                        r=["S_rw", ("REP", rb)], w=["TMPA"])
                    P.op("dve", lambda e, rb=rb, t=t: e.tensor_reduce(out=Wyv(rb, t), in_=TMPA[:], axis=AX.X, op=ALU.add),
                        r=["TMPA"], w=[("Wt", rb)])
                P.op("act", lambda e, rb=rb, col=col, nb=nb: e.copy(R1[:, 4:6, col:col + nb], Wt[:, rb, 0:nb, :, 2].rearrange("p t c -> p c t")),
                     r=[("Wt", rb)], w=["R1"])
            if grp["last"]:
                P.dma("sp", lambda e, sq=sq: e.dma_start(out=rw_out[l, sq].rearrange("c p k -> p c k"), in_=S_rw[:, sq]),
                      r=["S_rw"], w=["rw_out"], sem="d_o6")
        for c in range(2):
            Y = R1[:, 4 + c, :N]
            P.op("pe", lambda e, Y=Y: e.matmul(pC[:, :N], lhsT=blockones, rhs=Y, start=True, stop=True), r=["R1", "CST"], w=["pC"])
            P.op("dve", lambda e, Y=Y: e.scalar_tensor_tensor(out=GT[:, 0, :N], in0=pC[:, :N], scalar=-1.0 / 64, in1=Y, op0=ALU.mult, op1=ALU.add),
                 r=["pC", "R1"], w=["GT"])
            P.op("act", lambda e: e.activation(out=GT[:, 1, :N], in_=GT[:, 0, :N], func=AF.Square), r=["GT"], w=["GT"])
            P.op("pe", lambda e: e.matmul(pD[:, :N], lhsT=blockones, rhs=GT[:, 1, :N], start=True, stop=True), r=["GT", "CST"], w=["pD"])
            P.op("dve", lambda e: e.tensor_scalar(out=GT[:, 1, :N], in0=pD[:, :N], scalar1=1.0 / 64, scalar2=GN_EPS, op0=ALU.mult, op1=ALU.add),
                 r=["pD"], w=["GT"])
            P.op("act", lambda e: e.activation(out=GT[:, 1, :N], in_=GT[:, 1, :N], func=AF.Sqrt), r=["GT"], w=["GT"])
            P.op("dve", lambda e: e.reciprocal(GT[:, 1, :N], GT[:, 1, :N]), r=["GT"], w=["GT"])
            P.op("dve", lambda e: e.tensor_tensor(out=GT[:, 0, :N], in0=GT[:, 0, :N], in1=GT[:, 1, :N], op=ALU.mult), r=["GT"], w=["GT"])
            P.op("act", lambda e, c=c: e.activation(out=GT[:, 0, :N], in_=GT[:, 0, :N], func=AF.Identity,
                                                    scale=pc("lnx_g", l, c), bias=pc("lnx_b", l, c)), r=["GT", "PRM"], w=["GT"])
            P.op("dve", lambda e, c=c: e.tensor_tensor(out=GT[:, 1, :N], in0=Q5[:, c, 0, :N], in1=Q5[:, c, 3, :N], op=ALU.mult), r=["Q5", "GT"], w=["GT"])
            P.op("dve", lambda e, c=c: e.tensor_scalar(out=GT[:, 1, :N], in0=GT[:, 1, :N], scalar1=pc("rw_rk", l, c), scalar2=None, op0=ALU.mult),
                 r=["GT", "PRM"], w=["GT"])
            P.op("pe", lambda e: e.matmul(pC[:, :N], lhsT=blockones, rhs=GT[:, 1, :N], start=True, stop=True), r=["GT", "CST"], w=["pC"])
            P.op("dve", lambda e, c=c: e.tensor_tensor(out=GT[:, 1, :N], in0=pC[:, :N], in1=XR[:, 4 + c, :N], op=ALU.mult), r=["pC", "XR"], w=["GT"])
            P.op("dve", lambda e: e.tensor_tensor(out=GT[:, 0, :N], in0=GT[:, 0, :N], in1=GT[:, 1, :N], op=ALU.add), r=["GT"], w=["GT"])
            P.op("dve", lambda e, c=c: e.tensor_tensor(out=MIX[:, c, :N], in0=GT[:, 0, :N], in1=R1[:, 2 + c, :N], op=ALU.mult), r=["GT", "R1"], w=["MIX"])

    UPAD = sb("UPAD", [128, 2, GN + 2])
    GT = sb("GT", [128, 2, GN])
    VCM = sb("VCM", [128, 2, GN])
    UCM = sb("UCM", [128, 2, GN])
    VTMz = sb("VTMz", [128, 2, 2, 128])
    WMT = sb("WMT", [128, 4, 128])
    BSr = sb("BSr", [1, 512])
    P.op("dve", lambda e: e.memset(VTMz[:], 0.0), w=["VTMz"])

    NBLK = max(T // 256, 1)
    KT = sb("KT", [128, 2, T], BF16)
    QB = sb("QB", [128, 2, GN], BF16)
    V1 = sb("V1", [128, NTILE, 4, 64], BF16)
    ONESB = sb("ONESB", [128, 64], BF16)
    KMT = sb("KMT", [128, 2, 16])
    KN2 = sb("KN2", [128, 2])
    QSQ = sb("QSQ", [128, 2, GN])
    G16 = sb("G16", [128, 16])
    M8 = sb("M8", [128, 8])
    BTQ = sb("BTQ", [128, 16])
    MC = sb("MC", [128, 4])
    NEGH = sb("NEGH", [128, 1])
    ET = sb("ET", [128, 2, 4, 128], BF16)
    RZ = sb("RZ", [128, 128])
    P.op("dve", lambda e: e.memset(ONESB[:], 1.0), w=["ONESB"])
    P.op("dve", lambda e: e.memset(NEGH[:], -BIG / 2), w=["NEGH"])
    et_i = [0]

    def moba_prompt(l, grp):
        N = grp["N"]
        g = grp["g"]
        gc = grp["gcol"]
        own = g
        if g == 0:
            P.op("dve", lambda e: e.memset(KN2[:], 0.0), w=["KN2"])
        P.op("act", lambda e: e.copy(KT[:, :, gc:gc + N], PJ[:, 2:4, :N]), r=["PJ"], w=["KT"])
        P.op("act", lambda e: e.copy(QB[:, :, :N], PJ[:, 0:2, :N]), r=["PJ"], w=["QB"])
        for c in range(2):
            P.op("dve", lambda e, c=c: e.tensor_reduce(out=MC[:, 0:1], in_=PJ[:, 2 + c, :N], axis=AX.X, op=ALU.add), r=["PJ"], w=["MC"])
            P.op("act", lambda e, c=c: e.mul(KMT[:, c, own:own + 1], MC[:, 0:1], 1.0 / 256), r=["MC"], w=["KMT"])
            P.op("act", lambda e, c=c: e.activation(out=QSQ[:, c, :N], in_=PJ[:, 2 + c, :N], func=AF.Square), r=["PJ"], w=["QSQ"])
            P.op("pe", lambda e, c=c: e.matmul(pC[:, :N], lhsT=blockones, rhs=QSQ[:, c, :N], start=True, stop=True), r=["QSQ", "CST"], w=["pC"])
            P.op("dve", lambda e: e.tensor_reduce(out=MC[:, 1:2], in_=pC[:, :N], axis=AX.X, op=ALU.max), r=["pC"], w=["MC"])
            P.op("dve", lambda e, c=c: e.tensor_tensor(out=KN2[:, c:c + 1], in0=KN2[:, c:c + 1], in1=MC[:, 1:2], op=ALU.max), r=["MC", "KN2"], w=["KN2"])
            for ti in range(N // 128):
                tile = gc // 128 + ti
                P.op("pe", lambda e, c=c, ti=ti: e.transpose(pD[:, 0:128], PJ[:, 4 + c, ti * 128:(ti + 1) * 128], ident), r=["PJ", "CST"], w=["pD"])
                P.op("act", lambda e, c=c, tile=tile: e.copy(V1[:, tile, 2 * c:2 * c + 2, :].rearrange("p h d -> p (h d)"), pD[:, 0:128]),
                     r=["pD"], w=["V1"])
        for c in range(2):
            P.op("act", lambda e, c=c: e.activation(out=QSQ[:, c, :N], in_=PJ[:, c, :N], func=AF.Square), r=["PJ", "QSQ"], w=["QSQ"])
        for h in range(4):
            c, hh = h // 2, h % 2
            PH = slice(hh * 64, hh * 64 + 64)
            for ti in range(N // 128):
                qi = gc // 128 + ti
                qs = slice(ti * 128, (ti + 1) * 128)
                def mm0(e, c=c, PH=PH, qs=qs):
                    if own > 0:
                        e.matmul(pC[:, 0:own], lhsT=PJ[PH, c, qs], rhs=KMT[PH, c, 0:own], start=True, stop=True)
                    e.matmul(pC[:, 16:17], lhsT=QSQ[PH, c, qs], rhs=ones[PH, 0:1], start=True, stop=True)
                    return e.matmul(pC[:, 17:18], lhsT=ones[PH, :], rhs=KN2[PH, c:c + 1], start=True, stop=True)
                P.op("pe", mm0, r=["PJ", "KMT", "QSQ", "KN2", "CST"], w=["pC"])
                P.op("dve", lambda e: e.scalar_tensor_tensor(out=MC[:, 2:3], in0=pC[:, 16:17], scalar=1.0 / 64, in1=pC[:, 17:18],
                                                             op0=ALU.mult, op1=ALU.mult) if False else
                     e.tensor_scalar(out=MC[:, 2:3], in0=pC[:, 16:17], scalar1=1.0 / 64, scalar2=None, op0=ALU.mult), r=["pC"], w=["MC"])
                P.op("dve", lambda e: e.tensor_tensor(out=MC[:, 2:3], in0=MC[:, 2:3], in1=pC[:, 17:18], op=ALU.mult), r=["pC", "MC"], w=["MC"])
                P.op("act", lambda e: e.activation(out=MC[:, 2:3], in_=MC[:, 2:3], func=AF.Sqrt), r=["MC"], w=["MC"])
                P.op("dve", lambda e: e.tensor_scalar(out=MC[:, 3:4], in0=MC[:, 2:3], scalar1=-1.0, scalar2=-8.0 * BIG, op0=ALU.mult, op1=ALU.add),
                     r=["MC"], w=["MC"])
                P.op("dve", lambda e: e.tensor_scalar(out=MC[:, 2:3], in0=MC[:, 2:3], scalar1=-1.0, scalar2=None, op0=ALU.mult), r=["MC"], w=["MC"])
                if own > 0:
                    P.op("dve", lambda e: e.memset(G16[:], -BIG), r=["BTQ"], w=["G16"])
                    P.op("dve", lambda e: e.tensor_copy(G16[:, 0:own], pC[:, 0:own]), r=["pC"], w=["G16"])
                    if own > NSEL:
                        P.op("dve", lambda e: e.max(out=M8[:], in_=G16[:]), r=["G16"], w=["M8"])
                        thr = M8[:, NSEL - 1:NSEL]
                    else:
                        thr = NEGH[:, 0:1]
                    P.op("dve", lambda e, thr=thr: e.tensor_scalar(out=BTQ[:], in0=G16[:], scalar1=thr, scalar2=None, op0=ALU.is_ge),
                         r=["G16", "M8", "NEGH"], w=["BTQ"])
                    P.op("dve", lambda e: e.tensor_scalar(out=BTQ[:], in0=BTQ[:], scalar1=8.0 * BIG, scalar2=MC[:, 3:4], op0=ALU.mult, op1=ALU.add),
                         r=["BTQ", "MC"], w=["BTQ"])
                P.op("dve", lambda e: e.tensor_copy(BTQ[:, own:own + 1], MC[:, 2:3]), r=["MC", "BTQ"], w=["BTQ"])
                kts = list(range(qi + 1))
                for b0 in range(0, len(kts), 4):
                    sub = kts[b0:b0 + 4]
                    eb = et_i[0]
                    et_i[0] = 1 - eb
                    pt, pk = (pE, "pE") if eb == 0 else (pF, "pF")

                    def mms(e, sub=sub, pt=pt, c=c, PH=PH, qs=qs):
                        for si, kt in enumerate(sub):
                            j = kt // 2
                            e.matmul(pt[:, si * 128:(si + 1) * 128], lhsT=KT[PH, c, kt * 128:(kt + 1) * 128], rhs=QB[PH, c, qs],
                                     start=True, stop=False)
                            i = e.matmul(pt[:, si * 128:(si + 1) * 128], lhsT=BTQ[:, j:j + 1].to_broadcast([128, 128]), rhs=ident,
                                         start=False, stop=True)
                        return i
                    P.op("pe", mms, r=["KT", "QB", "BTQ", "CST"], w=[pk])
                    ns = len(sub)
                    P.op("act", lambda e, pt=pt, eb=eb, ns=ns: e.activation(out=ET[:, eb, 0:ns, :].rearrange("p s q -> p (s q)"),
                                                                           in_=pt[:, 0:ns * 128], func=AF.Exp, scale=0.125),
                         r=[pk], w=[("ET", eb)])
                    if sub[-1] == qi:
                        si = len(sub) - 1
                        P.op("dve", lambda e, eb=eb, si=si: e.tensor_tensor(out=ET[:, eb, si, :], in0=ET[:, eb, si, :], in1=cs("tri"), op=ALU.mult),
                             r=[("ET", eb), "CST"], w=[("ET", eb)])

                    def mmv(e, sub=sub, eb=eb, h=h, PH=PH):
                        for si, kt in enumerate(sub):
                            e.matmul(pA[PH, 0:128], lhsT=V1[:, kt, h, :], rhs=ET[:, eb, si, :], start=(kt == 0), stop=(kt == qi))
                            i = e.matmul(pB[PH, 0:128], lhsT=ONESB[:], rhs=ET[:, eb, si, :], start=(kt == 0), stop=(kt == qi))
                        return i
                    P.op("pe", mmv, r=["V1", ("ET", eb), "ONESB"], w=["pA", "pB"])
                P.op("dve", lambda e, PH=PH: e.reciprocal(RZ[PH, :], pB[PH, 0:128]), r=["pB"], w=["RZ"])
                P.op("dve", lambda e, PH=PH, c=c, qs=qs: e.tensor_tensor(out=MIX[PH, 2 + c, qs], in0=pA[PH, 0:128], in1=RZ[PH, :], op=ALU.mult),
                     r=["pA", "RZ"], w=["MIX"])

    XMB = sb("XMB", [128, 8, GN], BF16)
    UTb = [sb(f"UTb{i}", [128, 8, 128], BF16) for i in range(3)]
    Vb = [sb(f"Vb{i}", [128, 1024], BF16) for i in range(3)]
    AG = [sb(f"AG{i}", [128, GN]) for i in range(2)]
    GAb = [sb(f"GAb{i}", [128, GN], BF16) for i in range(2)]
    V16 = sb("V16", [128, 16, 16])
    B16 = sb("B16", [128, 8, 16])
    TMP1 = sb("TMP1", [128, 128])
    CANDh = sb("CANDh", [128, 256])
    CAND2 = sb("CAND2", [128, 256])
    TAU = sb("TAU", [128, 2, 8])
    BIASH = sb("BIASH", [128, 2, 8])
    MXZ = sb("MXZ", [128, 3, 8])
    REPf = REP[:].rearrange("p a b c -> p (a b c)")
    QTv = REPf[:, 0:16 * GN].rearrange("p (h n) -> p h n", h=16)
    KEYSv = DG[:].rearrange("p a b c -> p (a b c)")[:, 0:2048].rearrange("p (h k) -> p h k", h=16)
    Q5f = Q5[:].rearrange("p a b c -> p (a b c)")
    DtE = [(Q5f[:, 0:1024].rearrange("p (a k) -> p a k", a=8), Q5f[:, 1024:2048].rearrange("p (a k) -> p a k", a=8), "Q5"),
           (XR[:].rearrange("p a b -> p (a b)")[:, 0:1024].rearrange("p (a k) -> p a k", a=8),
            PJ[:].rearrange("p a b -> p (a b)")[:, 0:1024].rearrange("p (a k) -> p a k", a=8), "XRPJ")]
    Sv = [R1[:].rearrange("p a b -> p (a b)").rearrange("p (h k) -> p h k", h=16) if GN == 256 else None,
          SQ[:].rearrange("p a b -> p (a b)").rearrange("p (h k) -> p h k", h=16) if GN == 256 else None]
    Skey = ["R1", "SQ"]
    GBv = MIX[:].rearrange("p a b -> p (a b)").rearrange("p (t a k) -> p t a k", t=2, a=8)
    REPK = [("REP", 0), ("REP", 1)]
    uv_i = [0]

    FENCE = sb("FENCE", [128, 2])
    PSUB = [("pE", 0), ("pE", 1), ("pF", 0), ("pF", 1), "Q5D", "Q5E", "XRPJD", "XRPJE"]
    PWHOLE = ["pE", "pF", "Q5", "XR", "PJ"]

    def peer(l, grp):
        N = grp["N"]
        NT = (N + 127) // 128
        P.op("dve", lambda e: e.memset(FENCE[:, 0:1], 0.0), r=PWHOLE, w=PSUB)
        tns = [min(128, N - tt * 128) for tt in range(NT)]
        P.op("act", lambda e: e.copy(XMB[:, :, :N], XM[:, :, :N]), r=["XM"], w=["XMB"])
        P.dma("sp", lambda e: e.dma_start(out=KEYSv, in_=keysT_in[l]), w=["DG"], sem="d_keys")

        def sink_q(j, pt, pk):
            P.op("act", lambda e, j=j, pt=pt: e.copy(QTv[:, j, :N], pt[:, :N]), r=[pk], w=REPK)
        proj(l, wq_in, 0, 16, XM, "XM", N, sink_q)
        for tt in range(NT):
            tn = tns[tt]
            S = Sv[tt]
            sk = Skey[tt]
            for hb in range(4):
                pt, pk = (pC, "pC") if hb % 2 == 0 else (pD, "pD")

                def mm(e, hb=hb, pt=pt, tt=tt, tn=tn):
                    for q in range(4):
                        hc = hb * 4 + q
                        i = e.matmul(pt[:tn, q * 128:(q + 1) * 128], lhsT=QTv[:, hc, tt * 128:tt * 128 + tn], rhs=KEYSv[:, hc, :],
                                     start=True, stop=True)
                    return i
                P.op("pe", mm, r=REPK + ["DG"], w=[pk])
                P.op("act", lambda e, hb=hb, pt=pt, S=S, tn=tn: e.copy(S[:tn, hb * 4:(hb + 1) * 4, :].rearrange("p h k -> p (h k)"), pt[:tn, :]),
                     r=[pk], w=[sk])
            for hc in range(16):
                P.op("dve", lambda e, hc=hc, S=S, tn=tn: e.max(out=V16[:tn, hc, 0:8], in_=S[:tn, hc, :]), r=[sk], w=["V16"])
                P.op("dve", lambda e, hc=hc, S=S, tn=tn: e.match_replace(out=TMP1[:tn, :], in_to_replace=V16[:tn, hc, 0:8], in_values=S[:tn, hc, :],
                                                                       imm_value=-BIG), r=[sk, "V16"], w=["TMP1"])
                P.op("dve", lambda e, hc=hc, tn=tn: e.max(out=V16[:tn, hc, 8:16], in_=TMP1[:tn, :]), r=["TMP1"], w=["V16"])
            for h in range(8):
                P.op("dve", lambda e, h=h, tn=tn: e.tensor_tensor(
                    out=CANDh[:tn, :].rearrange("p (a b) -> p a b", a=16),
                    in0=V16[:tn, 2 * h, :].unsqueeze(2).to_broadcast([tn, 16, 16]),
                    in1=V16[:tn, 2 * h + 1, :].unsqueeze(1).to_broadcast([tn, 16, 16]), op=ALU.add), r=["V16"], w=["CANDh"])
                P.op("dve", lambda e, h=h, tn=tn: e.max(out=B16[:tn, h, 0:8], in_=CANDh[:tn, :]), r=["CANDh"], w=["B16"])
                P.op("dve", lambda e, h=h, tn=tn: e.match_replace(out=CAND2[:tn, :], in_to_replace=B16[:tn, h, 0:8], in_values=CANDh[:tn, :],
                                                                 imm_value=-BIG), r=["CANDh", "B16"], w=["CAND2"])
                P.op("dve", lambda e, h=h, tn=tn: e.max(out=B16[:tn, h, 8:16], in_=CAND2[:tn, :]), r=["CAND2"], w=["B16"])
            P.op("dve", lambda e, tt=tt, tn=tn: e.tensor_reduce(out=TAU[:tn, tt, :], in_=B16[:tn], axis=AX.X, op=ALU.min), r=["B16"], w=["TAU"])
            P.op("dve", lambda e, tn=tn: e.tensor_reduce(out=MXZ[:tn, 0, :], in_=B16[:tn], axis=AX.X, op=ALU.max), r=["B16"], w=["MXZ"])
            P.op("dve", lambda e, tn=tn: e.tensor_tensor(out=B16[:tn], in0=B16[:tn], in1=MXZ[:tn, 0, :].unsqueeze(2).to_broadcast([tn, 8, 16]),
                                                         op=ALU.subtract), r=["B16", "MXZ"], w=["B16"])
            P.op("act", lambda e, tn=tn: e.activation(out=B16[:tn], in_=B16[:tn], func=AF.Exp), r=["B16"], w=["B16"])
            P.op("dve", lambda e, tn=tn: e.tensor_reduce(out=MXZ[:tn, 1, :], in_=B16[:tn], axis=AX.X, op=ALU.add), r=["B16"], w=["MXZ"])
            P.op("act", lambda e, tn=tn: e.activation(out=MXZ[:tn, 2, :], in_=MXZ[:tn, 1, :], func=AF.Ln), r=["MXZ"], w=["MXZ"])
            P.op("dve", lambda e, tt=tt, tn=tn: e.scalar_tensor_tensor(out=BIASH[:tn, tt, :], in0=MXZ[:tn, 0, :], scalar=-1.0, in1=MXZ[:tn, 2, :],
                                                                      op0=ALU.mult, op1=ALU.subtract), r=["MXZ"], w=["BIASH"])
        def load_uv(i1):
            sl = uv_i[0]
            uv_i[0] = (sl + 1) % 3
            e0 = i1 * 128
            P.dma("pool", lambda e, sl=sl, e0=e0: e.dma_start(out=UTb[sl][:], in_=UT_in[l, :, :, e0:e0 + 128]), w=[("UTb", sl)], sem=f"d_ut{sl}")
            P.dma("pool", lambda e, sl=sl, e0=e0: e.dma_start(out=Vb[sl][:], in_=V_in[l, e0:e0 + 128, :]), w=[("Vb", sl)], sem=f"d_vb{sl}")
            return sl
        slots = {}
        slots[0] = load_uv(0)
        slots[1] = load_uv(1)
        de_i = 0
        for blk in range(16):
            for tt in range(NT):
                tn = tns[tt]
                S = Sv[tt]
                sk = Skey[tt]
                for h in range(8):
                    Dt, Et, dk = DtE[de_i % 2]
                    de_i += 1
                    P.op("pool", lambda e, Dt=Dt, S=S, h=h, blk=blk, tn=tn: e.tensor_tensor(
                        out=Dt[:tn], in0=S[:tn, 2 * h, blk * 8:(blk + 1) * 8].unsqueeze(2).to_broadcast([tn, 8, 128]),
                        in1=S[:tn, 2 * h + 1, :].unsqueeze(1).to_broadcast([tn, 8, 128]), op=ALU.add), r=[sk], w=[dk + "D"])
                    P.op("act", lambda e, Dt=Dt, Et=Et, tt=tt, h=h, tn=tn: e.activation(out=Et[:tn], in_=Dt[:tn], func=AF.Exp,
                                                                                     bias=BIASH[:tn, tt, h:h + 1], scale=1.0),
                         r=[dk + "D", "BIASH"], w=[dk + "E"])
                    if h == 0:
                        P.op("dve", lambda e, Dt=Dt, Et=Et, tt=tt, h=h, tn=tn: e.scalar_tensor_tensor(
                            out=GBv[:tn, tt], in0=Dt[:tn], scalar=TAU[:tn, tt, h:h + 1], in1=Et[:tn], op0=ALU.is_ge, op1=ALU.mult),
                            r=[dk + "D", dk + "E", "TAU"], w=["MIX"])
                    else:
                        P.op("dve", lambda e, Dt=Dt, Et=Et, tt=tt, h=h, tn=tn: e.scalar_tensor_tensor(
                            out=Dt[:tn], in0=Dt[:tn], scalar=TAU[:tn, tt, h:h + 1], in1=Et[:tn], op0=ALU.is_ge, op1=ALU.mult),
                            r=[dk + "D", dk + "E", "TAU"], w=[dk + "D"])
                        P.op("pool", lambda e, Dt=Dt, tt=tt, tn=tn: e.tensor_tensor(out=GBv[:tn, tt], in0=GBv[:tn, tt], in1=Dt[:tn], op=ALU.add),
                             r=[dk + "D", "MIX"], w=["MIX"])
            for a_ in range(8):
                i1 = blk * 8 + a_
                if i1 + 2 < 128:
                    slots[i1 + 2] = load_uv(i1 + 2)
                sl = slots.pop(i1)
                pb = i1 % 2
                hs = slice(pb * 256, pb * 256 + N)

                def trs(e, a_=a_, pb=pb):
                    for tt in range(NT):
                        tn = tns[tt]
                        i = e.transpose(pF[:, pb * 256 + tt * 128:pb * 256 + tt * 128 + tn], GBv[:tn, tt, a_, :], ident[:tn, :tn])
                    return i
                P.op("pe", trs, r=["MIX", "CST"], w=[("pF", pb)])

                def mma(e, sl=sl, hs=hs):
                    for k in range(8):
                        i = e.matmul(pE[:, hs], lhsT=UTb[sl][:, k, :], rhs=XMB[:, k, :N], start=(k == 0), stop=(k == 7))
                    return i
                P.op("pe", mma, r=[("UTb", sl), "XMB"], w=[("pE", pb)])
                P.op("act", lambda e, pb=pb, hs=hs: e.activation(out=AG[pb][:, :N], in_=pE[:, hs], func=AF.Gelu_apprx_tanh),
                     r=[("pE", pb)], w=[("AG", pb)])
                P.op("dve", lambda e, pb=pb, hs=hs: e.tensor_tensor(out=GAb[pb][:, :N], in0=AG[pb][:, :N], in1=pF[:, hs], op=ALU.mult),
                     r=[("AG", pb), ("pF", pb)], w=[("GAb", pb)])

                def mmo(e, sl=sl, pb=pb, i1=i1):
                    for j in range(8):
                        pt = (pA, pB, pC, pD)[j // 2]
                        i = e.matmul(pt[:, (j % 2) * 256:(j % 2) * 256 + N], lhsT=Vb[sl][:, j * 128:(j + 1) * 128], rhs=GAb[pb][:, :N],
                                     start=(i1 == 0 and j % 2 == 0), stop=(i1 == 127))
                    return i
                P.op("pe", mmo, r=[("Vb", sl), ("GAb", pb)], w=["pA", "pB", "pC", "pD"])
        P.op("dve", lambda e: e.memset(FENCE[:, 1:2], 0.0), r=PSUB, w=PWHOLE)
        for (sq, c0, n, _) in grp["segs"]:
            for j in range(8):
                pt = (pA, pB, pC, pD)[j // 2]
                pk = ("pA", "pB", "pC", "pD")[j // 2]
                P.op("act", lambda e, j=j, pt=pt, sq=sq, c0=c0, n=n: e.activation(
                    out=R1[:, j, c0:c0 + n], in_=pt[:, (j % 2) * 256 + c0:(j % 2) * 256 + c0 + n], func=AF.Identity,
                    scale=MT1[:, 5 * 8 + j, sq:sq + 1]), r=[pk, "MT"], w=["R1"])

    DGf = DG[:].rearrange("p a b c -> p (a b c)")
    KPp = [REPf[:, 0:256], REPf[:, 256:512]]
    VPp = [REPf[:, 512:768], REPf[:, 768:1024]]
    KSQs = REPf[:, 1024:1280]
    RMs = DGf[:, 0:4]
    KMTs = DGf[:, 8:72].rearrange("p (c j) -> p c j", c=2)
    G32 = DGf[:8, 72:104]
    BTQs = DGf[:8, 104:236].rearrange("p (h j) -> p h j", h=4)
    KMXQ = DGf[:8, 240:244]
    TQ4 = DGf[:, 244:248]
    MCs = DGf[:, 248:256]
    DG4 = DGf[:4, 256:260]
    M8s = DGf[:8, 264:272]
    RZs = DGf[:, 272:288]
    SUBK = [("KP", 0), ("KP", 1), ("VP", 0), ("VP", 1), "KSQs", ("KTp", 0), ("KTp", 1), ("VPb", 0), ("VPb", 1),
            ("ETs", 0), ("ETs", 1), "MSM", "VN", "ET8"]
    WHOLEK = [("REP", 0), ("REP", 1), "DG", "KT", "V1", ("ET", 0), ("ET", 1)]

    rg_i = [0]

    def moba_sample(l, grp):
        N = grp["N"]
        P.op("dve", lambda e: e.memset(FENCE[:, 0:1], 0.0), r=[], w=WHOLEK + SUBK)
        P.op("act", lambda e: e.copy(QB[:, :, 0:N], PJ[:, 0:2, :N]), r=["PJ"], w=["QB"])
        P.op("act", lambda e: e.copy(QB[:, :, N:2 * N], PJ[:, 2:4, :N]), r=["PJ"], w=["QB"])
        for c in range(2):
            P.op("act", lambda e, c=c: e.activation(out=QSQ[:, c, :N], in_=PJ[:, c, :N], func=AF.Square), r=["PJ"], w=["QSQ"])
        kp = [0]

        def seg_body(sq, c0, n):
            b = sq - 1
            qs = slice(c0, c0 + n)

            def load_page(src, dst, key, p, semn):
                idx = b * NPAGE + p
                P.dma("pool", lambda e, idx=idx, src=src, dst=dst: e.indirect_dma_start(
                    out=dst, out_offset=None, in_=src, in_offset=bass.IndirectOffsetOnAxis(ap=IDX[:, idx:idx + 1], axis=0),
                    element_offset=l * POOL * 128 * 256), r=["IDX"], w=[key], sem=semn)
            P.op("dve", lambda e: e.memset(RMs, 0.0), r=["MSM"], w=["MSM"])
            for p in range(NPAGE):
                s_ = kp[0] % 2
                kp[0] += 1
                load_page(ckr, KPp[s_], ("KP", s_), p, f"d_kp{s_}")
                j = p // 2

                def mm(e, s_=s_, p=p, j=j):
                    e.matmul(pC[:, j:j + 1], lhsT=KPp[s_][:, 0:128], rhs=ones[:, 0:1], start=(p % 2 == 0), stop=(p % 2 == 1))
                    return e.matmul(pD[:, j:j + 1], lhsT=KPp[s_][:, 128:256], rhs=ones[:, 0:1], start=(p % 2 == 0), stop=(p % 2 == 1))
                P.op("pe", mm, r=[("KP", s_), "CST"], w=["pC", "pD"])
                P.op("act", lambda e, s_=s_: e.activation(out=KSQs, in_=KPp[s_], func=AF.Square), r=[("KP", s_)], w=["KSQs"])
                P.op("dve", lambda e: e.tensor_reduce(out=TQ4, in_=KSQs.rearrange("p (h d) -> p h d", h=4), axis=AX.X, op=ALU.add),
                     r=["KSQs"], w=["MSM"])
                P.op("dve", lambda e: e.tensor_tensor(out=RMs, in0=RMs, in1=TQ4, op=ALU.max), r=["MSM"], w=["MSM"])
            P.op("act", lambda e: e.mul(KMTs[:, 0, 0:NBLK_S], pC[:, 0:NBLK_S], 1.0 / 256), r=["pC"], w=["MSM"])
            P.op("act", lambda e: e.mul(KMTs[:, 1, 0:NBLK_S], pD[:, 0:NBLK_S], 1.0 / 256), r=["pD"], w=["MSM"])
            P.op("pe", lambda e: e.transpose(pE[:4, 0:128], RMs, ident), r=["MSM", "CST"], w=["pE"])
            P.op("dve", lambda e: e.tensor_reduce(out=MCs[:4, 0:1], in_=pE[:4, 0:128], axis=AX.X, op=ALU.max), r=["pE"], w=["MSM"])
            P.op("dve", lambda e: e.tensor_scalar(out=DG4, in0=ident[:4, :4], scalar1=MCs[:4, 0:1], scalar2=None, op0=ALU.mult),
                 r=["MSM", "CST"], w=["MSM"])
            P.op("pe", lambda e: e.matmul(pE[:8, 128:132], lhsT=ones[0:4, 0:8], rhs=DG4, start=True, stop=True), r=["MSM", "CST"], w=["pE"])
            P.op("act", lambda e: e.copy(KMXQ, pE[:8, 128:132]), r=["pE"], w=["MSM"])
            for h in range(4):
                c, hh = h // 2, h % 2
                PH = slice(hh * 64, hh * 64 + 64)

                def mm0(e, c=c, PH=PH):
                    e.matmul(pC[:8, 0:NBLK_S], lhsT=PJ[PH, c, qs], rhs=KMTs[PH, c, 0:NBLK_S], start=True, stop=True)
                    return e.matmul(pC[:8, 32:33], lhsT=QSQ[PH, c, qs], rhs=ones[PH, 0:1], start=True, stop=True)
                P.op("pe", mm0, r=["PJ", "MSM", "QSQ", "CST"], w=["pC"])
                P.op("dve", lambda e, h=h: e.tensor_tensor(out=MCs[:8, 2:3], in0=pC[:8, 32:33], in1=KMXQ[:, h:h + 1], op=ALU.mult),
                     r=["pC", "MSM"], w=["MSM"])
                P.op("act", lambda e: e.activation(out=MCs[:8, 2:3], in_=MCs[:8, 2:3], func=AF.Sqrt), r=["MSM"], w=["MSM"])
                P.op("dve", lambda e: e.tensor_scalar(out=MCs[:8, 3:4], in0=MCs[:8, 2:3], scalar1=-1.0, scalar2=-8.0 * BIG, op0=ALU.mult, op1=ALU.add),
                     r=["MSM"], w=["MSM"])
                P.op("dve", lambda e: e.tensor_scalar(out=MCs[:8, 2:3], in0=MCs[:8, 2:3], scalar1=-1.0, scalar2=None, op0=ALU.mult), r=["MSM"], w=["MSM"])
                P.op("dve", lambda e: e.memset(G32, -BIG), r=["MSM"], w=["MSM"])
                P.op("dve", lambda e: e.tensor_copy(G32[:, 0:NBLK_S], pC[:8, 0:NBLK_S]), r=["pC", "MSM"], w=["MSM"])
                if NBLK_S > NSEL:
                    P.op("dve", lambda e: e.max(out=M8s, in_=G32), r=["MSM"], w=["MSM"])
                    thr = M8s[:, NSEL - 1:NSEL]
                else:
                    thr = NEGH[:8, 0:1]
                P.op("dve", lambda e, h=h, thr=thr: e.tensor_scalar(out=BTQs[:, h, 0:NBLK_S], in0=G32[:, 0:NBLK_S], scalar1=thr, scalar2=None, op0=ALU.is_ge),
                     r=["MSM", "NEGH"], w=["MSM"])
                P.op("dve", lambda e, h=h: e.tensor_scalar(out=BTQs[:, h, 0:NBLK_S], in0=BTQs[:, h, 0:NBLK_S], scalar1=8.0 * BIG, scalar2=MCs[:8, 3:4],
                                                          op0=ALU.mult, op1=ALU.add), r=["MSM"], w=["MSM"])
                P.op("dve", lambda e, h=h: e.tensor_copy(BTQs[:, h, 32:33], MCs[:8, 2:3]), r=["MSM"], w=["MSM"])
            for p in range(NPAGE):
                s_ = kp[0] % 2
                kp[0] += 1
                j = p // 2
                load_page(ckr, KPp[s_], ("KP", s_), p, f"d_kp{s_}")
                load_page(cvr, VPp[s_], ("VP", s_), p, f"d_vp{s_}")
                KTp = KT[:, :, s_ * 128:(s_ + 1) * 128]
                VPb = V1[:, s_].rearrange("p h d -> p (h d)")
                ETs = ET[:, s_, 0, 0:32]

                def trk(e, s_=s_):
                    e.transpose(pD[:, 0:128], KPp[s_][:, 0:128], ident)
                    return e.transpose(pD[:, 128:256], KPp[s_][:, 128:256], ident)
                P.op("pe", trk, r=[("KP", s_), "CST"], w=["pD"])
                P.op("act", lambda e, KTp=KTp: e.copy(KTp, pD[:, 0:256].rearrange("p (c k) -> p c k", c=2)),
                     r=["pD"], w=[("KTp", s_)])
                P.op("pool", lambda e, VPb=VPb, s_=s_: e.tensor_copy(VPb, VPp[s_]), r=[("VP", s_)], w=[("VPb", s_)])

                def mms(e, KTp=KTp, j=j):
                    for h in range(4):
                        c, hh = h // 2, h % 2
                        PH = slice(hh * 64, hh * 64 + 64)
                        e.matmul(pE[:, h * 8:(h + 1) * 8], lhsT=KTp[PH, c, :], rhs=QB[PH, c, qs], start=True, stop=False)
                        i = e.matmul(pE[:, h * 8:(h + 1) * 8], lhsT=BTQs[:, h, j:j + 1].to_broadcast([8, 128]), rhs=ident[:8, :8],
                                     start=False, stop=True)
                    return i
                P.op("pe", mms, r=[("KTp", s_), "QB", "MSM", "CST"], w=["pE"])
                P.op("act", lambda e, ETs=ETs: e.activation(out=ETs, in_=pE[:, 0:32], func=AF.Exp, scale=0.125), r=["pE"], w=[("ETs", s_)])

                def mmv(e, VPb=VPb, ETs=ETs, p=p):
                    for h in range(4):
                        c, hh = h // 2, h % 2
                        PH = slice(hh * 64, hh * 64 + 64)
                        e.matmul(pA[PH, c * 8:(c + 1) * 8], lhsT=VPb[:, h * 64:(h + 1) * 64], rhs=ETs[:, h * 8:(h + 1) * 8],
                                 start=(p == 0 and c == 0), stop=False)
                        i = e.matmul(pB[PH, c * 8:(c + 1) * 8], lhsT=ONESB[:], rhs=ETs[:, h * 8:(h + 1) * 8],
                                     start=(p == 0 and c == 0), stop=False)
                    return i
                P.op("pe", mmv, r=[("VPb", s_), ("ETs", s_), "ONESB"], w=["pA", "pB"])
            VN = V1[:8, 2].rearrange("p h d -> p (h d)")
            ET8 = ET[:8, 0, 1, 0:32]

            def trv(e):
                e.transpose(pD[:8, 0:128], PJ[:, 4, qs], ident)
                return e.transpose(pD[:8, 128:256], PJ[:, 5, qs], ident)
            P.op("pe", trv, r=["PJ", "CST"], w=["pD"])
            P.op("act", lambda e, VN=VN: e.copy(VN, pD[:8, 0:256]), r=["pD"], w=["VN"])

            def mmo(e):
                for h in range(4):
                    c, hh = h // 2, h % 2
                    PH = slice(hh * 64, hh * 64 + 64)
                    e.matmul(pE[:8, 32 + h * 8:32 + (h + 1) * 8], lhsT=QB[PH, c, N + c0:N + c0 + n], rhs=QB[PH, c, qs], start=True, stop=False)
                    i = e.matmul(pE[:8, 32 + h * 8:32 + (h + 1) * 8], lhsT=BTQs[:, h, 32:33].to_broadcast([8, 8]), rhs=ident[:8, :8],
                                 start=False, stop=True)
                return i
            P.op("pe", mmo, r=["QB", "MSM", "CST"], w=["pE"])
            P.op("act", lambda e, ET8=ET8: e.activation(out=ET8, in_=pE[:8, 32:64], func=AF.Exp, scale=0.125), r=["pE"], w=["ET8"])
            P.op("dve", lambda e, ET8=ET8: e.tensor_tensor(out=ET8.rearrange("p (h q) -> p h q", h=4), in0=ET8.rearrange("p (h q) -> p h q", h=4),
                                                           in1=cs("tri")[:8, 0:8].unsqueeze(1).to_broadcast([8, 4, 8]), op=ALU.mult),
                 r=["ET8", "CST"], w=["ET8"])

            def mmv2(e, VN=VN, ET8=ET8):
                for h in range(4):
                    c, hh = h // 2, h % 2
                    PH = slice(hh * 64, hh * 64 + 64)
                    e.matmul(pA[PH, c * 8:(c + 1) * 8], lhsT=VN[:, h * 64:(h + 1) * 64], rhs=ET8[:, h * 8:(h + 1) * 8], start=False, stop=True)
                    i = e.matmul(pB[PH, c * 8:(c + 1) * 8], lhsT=ONESB[:8, :], rhs=ET8[:, h * 8:(h + 1) * 8], start=False, stop=True)
                return i
            P.op("pe", mmv2, r=["VN", "ET8", "ONESB"], w=["pA", "pB"])
            P.op("dve", lambda e: e.reciprocal(RZs, pB[:, 0:16]), r=["pB"], w=["MSM"])
            for c in range(2):
                P.op("dve", lambda e, c=c: e.tensor_tensor(out=MIX[:, 2 + c, qs], in0=pA[:, c * 8:(c + 1) * 8], in1=RZs[:, c * 8:(c + 1) * 8], op=ALU.mult),
                     r=["pA", "MSM"], w=["MIX"])

        for (sq_, c0_, n_, _) in grp["segs"]:
            seg_body(sq_, c0_, n_)
        P.op("dve", lambda e: e.memset(FENCE[:, 1:2], 0.0), r=[], w=WHOLEK + SUBK)

    def conv_mix(l, grp):
        N = grp["N"]
        for c in range(2):
            for (sq, c0, n, _) in grp["segs"]:
                P.op("act", lambda e, c=c, sq=sq: e.copy(UPAD[:, c, 0:2], CONV[:, sq, c, :]), r=["CONV"], w=["UPAD"])
                P.op("dve", lambda e, c=c, c0=c0, n=n: e.tensor_tensor(out=UPAD[:, c, 2:2 + n], in0=PJ[:, 2 + c, c0:c0 + n],
                                                                      in1=PJ[:, 4 + c, c0:c0 + n], op=ALU.mult), r=["PJ"], w=["UPAD"])
                P.op("dve", lambda e, c=c, n=n: e.tensor_scalar(out=GT[:, c, :n], in0=UPAD[:, c, 0:n], scalar1=pc("conv_w", l, c * 3 + 0),
                                                                scalar2=None, op0=ALU.mult), r=["UPAD", "PRM"], w=["GT"])
                for j in (1, 2):
                    P.op("dve", lambda e, c=c, n=n, j=j: e.scalar_tensor_tensor(
                        out=GT[:, c, :n], in0=UPAD[:, c, j:j + n], scalar=pc("conv_w", l, c * 3 + j), in1=GT[:, c, :n],
                        op0=ALU.mult, op1=ALU.add), r=["UPAD", "PRM", "GT"], w=["GT"])
                P.op("dve", lambda e, c=c, c0=c0, n=n: e.tensor_tensor(out=MIX[:, 4 + c, c0:c0 + n], in0=GT[:, c, :n],
                                                                      in1=PJ[:, c, c0:c0 + n], op=ALU.mult), r=["GT", "PJ"], w=["MIX"])
                P.op("act", lambda e, c=c, sq=sq, n=n: e.copy(CONV[:, sq, c, :], UPAD[:, c, n:n + 2]), r=["UPAD"], w=["CONV"])
        if grp["last"]:
            for (sq, c0, n, _) in grp["segs"]:
                P.dma("sp", lambda e, sq=sq: e.dma_start(out=cv_out[l, sq], in_=CONV[:, sq].rearrange("p a b -> p (a b)")),
                      r=["CONV"], w=["cv_out"], sem="d_o4")

    def cmlp_layer_setup(l):
        P.dma("sp", lambda e: e.dma_start(out=WMT[:], in_=cmws_in[l]), w=["WMT"], sem="d_cm")
        P.dma("sp", lambda e: e.dma_start(out=BSr[:], in_=cmbs_in[l]), w=["BSr"], sem="d_cm2")
        for h in range(4):
            P.op("dve", lambda e, h=h: e.tensor_tensor(out=WMT[:, h, :], in0=WMT[:, h, :], in1=cs("tri"), op=ALU.mult),
                 r=["WMT", "CST"], w=["WMT"])

    def cmlp_mix(l, grp):
        N = grp["N"]
        gelu_tanh(UCM[:, :, :N], PJ[:, 0:2, :N], ["PJ"], ["UCM"], GT[:, :, :N])
        gelu_tanh(VCM[:, :, :N], PJ[:, 2:4, :N], ["PJ"], ["VCM"], GT[:, :, :N])
        ln_fm(VCM, "VCM", VCM, "VCM", N, "cm_ln_g", "cm_ln_b", l, nch=2)
        if grp["g"] == -1:
            P.dma("sp", lambda e: e.dma_start(out=cm_out[l].rearrange("c p t -> p c t"), in_=VCM[:, :, :N]), r=["VCM"], w=["cm_out"], sem="d_o5")
        if grp["g"] >= 0:
            tchunks = [(t0, 128) for t0 in range(0, N, 128)]
        else:
            tchunks = [(c0, n) for (_, c0, n, _) in grp["segs"]]
        for (t0, n) in tchunks:
            for c in range(2):
                P.op("pe", lambda e, c=c, t0=t0, n=n: e.transpose(pC[:n, c * 128:(c + 1) * 128], VCM[:, c, t0:t0 + n], ident),
                     r=["VCM", "CST"], w=["pC"])
            for c in range(2):
                for hh in range(2):
                    P.op("act", lambda e, c=c, hh=hh, n=n: e.copy(VTMz[:n, c, hh, hh * 64:(hh + 1) * 64],
                                                                pC[:n, c * 128 + hh * 64:c * 128 + (hh + 1) * 64]), r=["pC"], w=["VTMz"])
            for c in range(2):
                def mm(e, c=c, n=n):
                    for hh in range(2):
                        h = 2 * c + hh
                        e.matmul(pD[:, c * 128:c * 128 + n], lhsT=VTMz[:n, c, hh, :], rhs=WMT[:n, h, :n], start=(hh == 0), stop=False)
                    for hh in range(2):
                        h = 2 * c + hh
                        i = e.matmul(pD[:, c * 128:c * 128 + n], lhsT=cs("halfones")[0:1, hh * 128:(hh + 1) * 128],
                                     rhs=BSr[0:1, h * 128:h * 128 + n], start=False, stop=(hh == 1))
                    return i
                P.op("pe", mm, r=["VTMz", "WMT", "BSr", "CST"], w=["pD"])
                P.op("dve", lambda e, c=c, t0=t0, n=n: e.tensor_tensor(out=MIX[:, 6 + c, t0:t0 + n], in0=pD[:, c * 128:c * 128 + n],
                                                                      in1=UCM[:, c, t0:t0 + n], op=ALU.mult), r=["pD", "UCM"], w=["MIX"])

    def group_body(l, grp, cur, nxt):
        N = grp["N"]
        gc = grp["gcol"]
        segs = grp["segs"]
        P.dma("sp", lambda e, gc=gc, N=N: e.dma_start(out=X[:, :, :N], in_=xres[cur, :, :, gc:gc + N].rearrange("c p t -> p c t")),
              r=[("xres", cur, gc)], w=["X"], sem="d_x")
        modulate(X, "X", XM, "XM", grp, 1, 0)
        if l == 0 and grp["g"] == 0:
            P.dma("sp", lambda e: e.dma_start(out=dbg2_out, in_=X[:]), r=["X"], w=["dbg2"], sem="d_dbg")

        if grp["first"]:
            for (sq, c0, n, _) in segs:
                if sq == 0:
                    P.op("dve", lambda e: e.memset(S_rw[:, 0], 0.0), w=["S_rw"])
                    P.op("dve", lambda e: e.memset(SHIFT[:, 0], 0.0), w=["SHIFT"])
                    P.op("dve", lambda e: e.memset(CONV[:, 0], 0.0), w=["CONV"])
                else:
                    b = sq - 1
                    P.dma("sp", lambda e, sq=sq, b=b: e.dma_start(out=S_rw[:, sq], in_=strw_in[l, b].rearrange("c p k -> p c k")),
                          w=["S_rw"], sem="d_st")
                    P.dma("sp", lambda e, sq=sq, b=b: e.dma_start(out=SHIFT[:, sq], in_=stsh_in[l, b]), w=["SHIFT"], sem="d_st")
                    P.dma("sp", lambda e, sq=sq, b=b: e.dma_start(out=CONV[:, sq].rearrange("p a b -> p (a b)"), in_=stcv_in[l, b]),
                          w=["CONV"], sem="d_st")

        P.op("dve", lambda e, N=N: e.memset(MIX[:, :, :N], 0.0), w=["MIX"])

        def sink_rw(j, pt, pk):
            P.op("act", lambda e, j=j, pt=pt: e.copy(PJ[:, j, :N], pt[:, :N]), r=[pk], w=["PJ"])
        proj(l, wmix_in, 0, 7, XM, "XM", N, sink_rw)
        for (sq, c0, n, _) in segs:
            P.op("dve", lambda e, sq=sq, c0=c0: e.tensor_tensor(out=XR[:, :, c0], in0=SHIFT[:, sq, :], in1=PJ[:, :, c0], op=ALU.subtract),
                 r=["SHIFT", "PJ"], w=["XR"])
            if n > 1:
                P.op("dve", lambda e, c0=c0, n=n: e.tensor_tensor(out=XR[:, :, c0 + 1:c0 + n], in0=PJ[:, :, c0:c0 + n - 1],
                                                                  in1=PJ[:, :, c0 + 1:c0 + n], op=ALU.subtract),
                     r=["PJ"], w=["XR"])
            P.op("act", lambda e, sq=sq, c0=c0, n=n: e.copy(SHIFT[:, sq, :], PJ[:, :, c0 + n - 1]),
                 r=["PJ", "XR"], w=["SHIFT"])
            if grp["last"]:
                P.dma("sp", lambda e, sq=sq: e.dma_start(out=sh_out[l, sq], in_=SHIFT[:, sq, :]), r=["SHIFT"], w=["sh_out"], sem="d_o1")
        for k in range(7):
            P.op("dve", lambda e, k=k: e.scalar_tensor_tensor(out=XR[:, k, :N], in0=XR[:, k, :N], scalar=pc("rw_mu", l, k),
                                                              in1=PJ[:, k, :N], op0=ALU.mult, op1=ALU.add),
                 r=["XR", "PJ", "PRM"], w=["XR"])
        if "rwkv" in stages:
            rwkv(l, grp)

        def sink_at(j, pt, pk):
            P.op("act", lambda e, j=j, pt=pt: e.copy(PJ[:, j, :N], pt[:, :N]), r=[pk], w=["PJ"])
        proj(l, wmix_in, O_AT, 6, XM, "XM", N, sink_at)
        P.dma("sp", lambda e, gc=gc, N=N: e.dma_start(out=kT_out[l, :, :, gc:gc + N].rearrange("c p t -> p c t"), in_=PJ[:, 2:4, :N]),
              r=["PJ"], w=["kT_out"], sem="d_o2")
        P.dma("sp", lambda e, gc=gc, N=N: e.dma_start(out=vT_out[l, :, :, gc:gc + N].rearrange("c p t -> p c t"), in_=PJ[:, 4:6, :N]),
              r=["PJ"], w=["vT_out"], sem="d_o2")

        if "moba" in stages and grp["g"] >= 0:
            moba_prompt(l, grp)
        if "mobas" in stages and grp["g"] == -1:
            moba_sample(l, grp)

        def sink_cv(j, pt, pk):
            P.op("act", lambda e, j=j, pt=pt: e.copy(PJ[:, j, :N], pt[:, :N]), r=[pk], w=["PJ"])
        proj(l, wmix_in, O_CV, 6, XM, "XM", N, sink_cv)
        if "conv" in stages:
            conv_mix(l, grp)

        def sink_cm(j, pt, pk):
            P.op("act", lambda e, j=j, pt=pt: e.copy(PJ[:, j, :N], pt[:, :N]), r=[pk], w=["PJ"])
        proj(l, wmix_in, O_CM, 4, XM, "XM", N, sink_cm)
        if "cmlp" in stages:
            cmlp_mix(l, grp)

        def sink_out(j, pt, pk):
            for (sq, c0, n, _) in segs:
                P.op("act", lambda e, j=j, pt=pt, sq=sq, c0=c0, n=n: e.activation(
                    out=R1[:, j, c0:c0 + n], in_=pt[:, c0:c0 + n], func=AF.Identity, scale=MT1[:, 2 * 8 + j, sq:sq + 1]),
                    r=[pk, "MT"], w=["R1"])
            P.op("dve", lambda e, j=j: e.scalar_tensor_tensor(out=R1[:, j, :N], in0=X[:, j, :N], scalar=ALPHA, in1=R1[:, j, :N],
                                                              op0=ALU.mult, op1=ALU.add), r=["X", "R1"], w=["R1"])
        if l == 0 and grp["g"] == -1:
            P.dma("sp", lambda e: e.dma_start(out=dbg3_out, in_=MIX[:, :, :TSA]), r=["MIX"], w=["dbg3"], sem="d_dbg")
        if l == 0 and grp["g"] == NPG - 1:
            P.dma("sp", lambda e: e.dma_start(out=dbg2_out, in_=MIX[:]), r=["MIX"], w=["dbg2"], sem="d_dbg")
        proj(l, wout_in, 0, 8, MIX, "MIX", N, sink_out)
        ln_fm(R1, "R1", X, "X", N, "ln1_g", "ln1_b", l)

        modulate(X, "X", XM, "XM", grp, 4, 3)
        if "peer" in stages:
            peer(l, grp)
        else:
            P.op("dve", lambda e, N=N: e.memset(R1[:, :, :N], 0.0), w=["R1"])
        for j in range(8):
            P.op("dve", lambda e, j=j: e.scalar_tensor_tensor(out=R1[:, j, :N], in0=X[:, j, :N], scalar=ALPHA, in1=R1[:, j, :N],
                                                              op0=ALU.mult, op1=ALU.add), r=["X", "R1"], w=["R1"])
        ln_fm(R1, "R1", XM, "XM", N, "ln2_g", "ln2_b", l)
        if l == L - 1:
            P.dma("sp", lambda e, gc=gc, N=N: e.dma_start(out=yT_out[:, :, gc:gc + N].rearrange("c p t -> p c t"), in_=XM[:, :, :N]),
                  r=["XM"], w=["yT_out"], sem="d_o3")
        else:
            P.dma("sp", lambda e, gc=gc, N=N: e.dma_start(out=xres[nxt, :, :, gc:gc + N].rearrange("c p t -> p c t"), in_=XM[:, :, :N]),
                  r=["XM"], w=[("xres", nxt, gc)], sem="d_xs")


    def adaln_setup(l):
        for jj in range(48):
            s = jj % 2
            P.dma("sp", lambda e, s=s, jj=jj: e.dma_start(out=WRA[s][:], in_=wada_in[l, :, :, jj * 128:(jj + 1) * 128]),
                  w=[("WRA", s)], sem=f"d_wra{s}")
            for q in range(1):
                j = jj

                def mm(e, s=s, q=q):
                    for k in range(8):
                        i = e.matmul(pC[:, :NSQ], lhsT=WRA[s][:, k, q * 128:(q + 1) * 128], rhs=CTs[:, k, :],
                                     start=(k == 0), stop=(k == 7))
                    return i
                P.op("pe", mm, r=[("WRA", s), "CTs"], w=["pC"])
                P.op("act", lambda e, j=j: e.activation(out=MT[:, j, :], in_=pC[:, :NSQ], func=AF.Identity,
                                                        bias=pc("b_ada", l, j), scale=1.0), r=["pC", "PRM"], w=["MT"])
        P.op("dve", lambda e: e.tensor_scalar(out=MT1[:], in0=MT[:], scalar1=1.0, scalar2=None, op0=ALU.add), r=["MT"], w=["MT"])

    for l in range(L):
        cur, nxt = l % 2, (l + 1) % 2
        adaln_setup(l)
        cmlp_layer_setup(l)
        rwkv_layer_setup(l)
        if l == 0:
            P.dma("sp", lambda e: e.dma_start(out=dbg_out, in_=MT[:]), r=["MT"], w=["dbg"], sem="d_dbg")
        for grp in groups:
            group_body(l, grp, cur, nxt)

    P.wait_all("sp")
    print("SBUF bytes remaining:", nc.sbuf_bytes_remaining, "ops:", P.n_ops, flush=True)
    P.emit()
    st.close()
    return nc


def _fm(a):
    n, f = a.shape
    return np.ascontiguousarray(a.T.reshape(f // 128, 128, n))


def _col(v):
    return np.ascontiguousarray(np.asarray(v, np.float32).reshape(-1, 128).T)


def make_in_maps(inp, cfg, ncores):
    L, T, NS, TS = cfg["L"], cfg["T"], cfg["NS"], cfg["TS"]
    PAST = cfg["PAST"]
    NPAGE = PAST // 128
    poff, NPRM = prm_layout(L)
    f = lambda k: np.asarray(inp[k], np.float32)
    prm = np.zeros((128, NPRM), np.float32)

    def put(name, l, v):
        c = _col(v)
        o = poff[name if l is None else (name, l)]
        prm[:, o:o + c.shape[1]] = c
    put("ln_in_g", None, f("ln_in_g"))
    put("ln_in_b", None, f("ln_in_b"))
    for l in range(L):
        put("b_ada", l, f("b_ada")[l])
        for nm in ("ln1_g", "ln1_b", "ln2_g", "ln2_b", "rw_mu", "rw_w0", "rw_a0", "rw_kk", "rw_ka"):
            put(nm, l, f(nm)[l])
        put("rw_rk", l, f("rw_rk")[l].reshape(-1))
        put("lnx_g", l, f("rw_lnx_g")[l])
        put("lnx_b", l, f("rw_lnx_b")[l])
        cw = f("conv_w")[l]
        o = poff[("conv_w", l)]
        prm[:, o:o + 6] = cw.reshape(3, 2, 128).transpose(2, 1, 0).reshape(128, 6)
        put("cm_ln_g", l, f("cm_ln_g")[l])
        put("cm_ln_b", l, f("cm_ln_b")[l])
    ctab, _ = const_table()
    r8 = lambda w: np.ascontiguousarray(w.reshape(L, 8, 128, -1).transpose(0, 2, 1, 3))
    shared = {
        "prm": prm, "cst": ctab,
        "w_ada": r8(f("w_ada")), "w_mix": r8(f("w_mix")), "w_out": r8(f("w_out")), "wq": r8(f("peer_wq")),
        "lrw": np.ascontiguousarray(np.concatenate([f("rw_w2"), f("rw_a2"), f("rw_g2")], axis=1)),
        "keysT": np.ascontiguousarray(f("peer_keys").reshape(L, 16, 128, 128).transpose(0, 3, 1, 2)),
        "cmwsT": np.ascontiguousarray(f("cm_ws").transpose(0, 3, 1, 2)),
        "cmbs": np.ascontiguousarray(f("cm_bs").reshape(L, 1, 512)),
        "UT": np.ascontiguousarray(f("peer_u").transpose(0, 2, 1).reshape(L, 8, 128, -1).transpose(0, 2, 1, 3)),
        "PV": f("peer_v"),
        "cache_k": f("cache_k").reshape(L, -1, 128, 256), "cache_v": f("cache_v").reshape(L, -1, 128, 256),
    }
    maps = []
    for c in range(ncores):
        bp = c // 2 if ncores == 8 else c % f("x_prompt").shape[0]
        sb_ = slice(NS * c, NS * c + NS)
        xs = f("x_sample")[sb_].reshape(NS * TS, D)
        xT = np.concatenate([_fm(f("x_prompt")[bp][:T]), _fm(xs)], axis=2)
        cc = np.concatenate([f("c_prompt")[bp:bp + 1], f("c_sample")[sb_]], axis=0)
        m = dict(shared)
        m["xT"] = np.ascontiguousarray(xT)
        m["cT"] = np.ascontiguousarray(cc.T.reshape(8, 128, -1).transpose(1, 0, 2))
        m["ptab"] = np.ascontiguousarray(np.asarray(inp["page_table"], np.int32)[sb_][:, :NPAGE].reshape(1, -1))
        m["st_rw"] = np.ascontiguousarray(f("state_rwkv")[:, sb_].reshape(L, NS, 2, 128, 64))
        m["st_sh"] = np.ascontiguousarray(f("state_shift")[:, sb_].reshape(L, NS, 7, 128).transpose(0, 1, 3, 2))
        m["st_cv"] = np.ascontiguousarray(f("state_conv")[:, sb_].reshape(L, NS, 2, 2, 128).transpose(0, 1, 4, 3, 2).reshape(L, NS, 128, 4))
        maps.append(m)
    return maps


def assemble(res, cfg, ncores, nbp, nbs):
    L, T, NS, TS = cfg["L"], cfg["T"], cfg["NS"], cfg["TS"]
    TSA = NS * TS
    tm = lambda a: a.reshape(-1, a.shape[-1]).T
    y_p = np.zeros((nbp, T, D), np.float32)
    y_s = np.zeros((nbs, TS, D), np.float32)
    k_p = np.zeros((L, nbp, T, NH, HD), np.float32)
    v_p = np.zeros_like(k_p)
    k_s = np.zeros((L, nbs, TS, NH, HD), np.float32)
    v_s = np.zeros_like(k_s)
    rw_p = np.zeros((L, nbp, NH, HD, HD), np.float32)
    rw_s = np.zeros((L, nbs, NH, HD, HD), np.float32)
    sh_p = np.zeros((L, nbp, RW_COLS), np.float32)
    sh_s = np.zeros((L, nbs, RW_COLS), np.float32)
    cv_p = np.zeros((L, nbp, 2, G), np.float32)
    cv_s = np.zeros((L, nbs, 2, G), np.float32)
    cm_s = np.zeros((L, nbs, TS, G), np.float32)
    for c in range(ncores):
        r = res[c]
        bp = c // 2 if ncores == 8 else c
        own_p = (ncores != 8) or (c % 2 == 0)
        yt = tm(r["yT"])
        if own_p:
            y_p[bp] = yt[:T]
        y_s[NS * c:NS * c + NS] = yt[T:].reshape(NS, TS, D)
        for l in range(L):
            kt, vt = tm(r["kT"][l]), tm(r["vT"][l])
            if own_p:
                k_p[l, bp] = kt[:T].reshape(T, NH, HD)
                v_p[l, bp] = vt[:T].reshape(T, NH, HD)
                rw_p[l, bp] = r["rw"][l, 0].reshape(NH, HD, HD)
                sh_p[l, bp] = r["sh"][l, 0].T.reshape(-1)
                cv_p[l, bp] = r["cv"][l, 0].reshape(128, 2, 2).transpose(2, 1, 0).reshape(2, G)
            k_s[l, NS * c:NS * c + NS] = kt[T:].reshape(NS, TS, NH, HD)
            v_s[l, NS * c:NS * c + NS] = vt[T:].reshape(NS, TS, NH, HD)
            for b in range(NS):
                rw_s[l, NS * c + b] = r["rw"][l, 1 + b].reshape(NH, HD, HD)
                sh_s[l, NS * c + b] = r["sh"][l, 1 + b].T.reshape(-1)
                cv_s[l, NS * c + b] = r["cv"][l, 1 + b].reshape(128, 2, 2).transpose(2, 1, 0).reshape(2, G)
            cm_s[l, NS * c:NS * c + NS] = tm(r["cm"][l]).reshape(NS, TS, G)
    return (y_p, y_s, k_p, v_p, k_s, v_s, rw_p, rw_s, sh_p, sh_s, cv_p, cv_s, cm_s)


def run(inp, cfg, ncores):
    nc = build(cfg)
    maps = make_in_maps(inp, cfg, ncores)
    res = run_bass_kernel_spmd(nc, maps, core_ids=list(range(ncores)))
    global LAST
    LAST = res.results
    nbp = 4 if ncores == 8 else ncores
    return assemble(res.results, cfg, ncores, nbp, cfg["NS"] * ncores)


def kernel(**inputs):
    cfg = default_cfg()
    return run(inputs, cfg, 8)
```
